# Optimizing a Trainium2 kernel written in Bass

```python
import jax, jax.numpy as jnp
from jax import lax
import numpy as np

D_MODEL = 1024
BATCH = 8
SEQ = 2048
DEPTH = 1
DEC_BATCH = 128
DEC_SEQ = 4
PAST_LEN = 16384
PAGE_SIZE = 128

D_MIX = D_MODEL
D_CONV = D_MIX // 2
CONV_HEADS = 8
D_POOL = D_MIX - D_CONV
POOL_WINDOWS = (2, 4, 8, 16)
N_POOL_GROUPS = len(POOL_WINDOWS)
POOL_GC = D_POOL // N_POOL_GROUPS
POOL_MAX = max(POOL_WINDOWS)
CONV_W = 3
D_IN = 4 * D_CONV + 2 * D_POOL
EPS = 1e-6

kernel_name = "hymba_conv_pool_decoder_step"


def _rmsnorm(x, g):
    xf = x.astype(jnp.float32)
    r = lax.rsqrt(jnp.mean(xf * xf, axis=-1, keepdims=True) + EPS)
    return (xf * r).astype(x.dtype) * g


def _mixer(h, conv_buf, pool_buf, n_past, w_in, conv_w, pool_w, pool_scale, w_out):
    bsz, s, _ = h.shape
    proj = jnp.einsum('bsd,de->bse', h, w_in)
    b_g, c_g, u, z_a, v, z_b = jnp.split(
        proj, [D_CONV, 2 * D_CONV, 3 * D_CONV, 4 * D_CONV, 4 * D_CONV + D_POOL], axis=-1)

    cu = c_g * u
    up = jnp.concatenate([conv_buf.astype(cu.dtype), cu], axis=1)
    yc = conv_w[0] * up[:, 0:s]
    for k in range(1, CONV_W):
        yc = yc + conv_w[k] * up[:, k:k + s]
    y_a = b_g * yc * jax.nn.silu(z_a)
    new_conv = up[:, -(CONV_W - 1):]

    vp = jnp.concatenate([pool_buf.astype(v.dtype), v], axis=1)
    vpf = vp.astype(jnp.float32)
    cs = jnp.concatenate([jnp.zeros((bsz, 1, D_POOL), jnp.float32),
                          jnp.cumsum(vpf, axis=1)], axis=1)
    cs_end = cs[:, POOL_MAX:POOL_MAX + s]
    pos = n_past + jnp.arange(s, dtype=jnp.int32) + 1
    means = []
    for gi, w in enumerate(POOL_WINDOWS):
        sl = slice(gi * POOL_GC, (gi + 1) * POOL_GC)
        wsum = cs_end[..., sl] - cs[:, POOL_MAX - w:POOL_MAX - w + s, sl]
        cnt = jnp.minimum(pos, w).astype(jnp.float32)[None, :, None]
        means.append(wsum / cnt)
    mean = jnp.concatenate(means, axis=-1)
    pooled = (mean - v.astype(jnp.float32)).astype(v.dtype)
    pooled = pooled.reshape(bsz, s, N_POOL_GROUPS, POOL_GC)
    pooled = jnp.einsum('bsgc,gcd->bsgd', pooled, pool_w).reshape(bsz, s, D_POOL)
    y_b = pooled * pool_scale * jax.nn.silu(z_b)
    new_pool = vp[:, -(POOL_MAX - 1):]

    y = jnp.einsum('bse,ed->bsd', jnp.concatenate([y_a, y_b], axis=-1), w_out)
    return y, new_conv, new_pool


def setup_inputs(seed: int = 0) -> dict:
    key = jax.random.key(seed)
    ks = jax.random.split(key, 12)
    f32 = jnp.float32
    x_prompt = jax.random.normal(ks[0], (BATCH, SEQ, D_MODEL), f32)
    x_sample = jax.random.normal(ks[1], (DEC_BATCH, DEC_SEQ, D_MODEL), f32)
    state_conv = jax.random.normal(ks[2], (DEPTH, DEC_BATCH, CONV_W - 1, D_CONV), f32)
    state_pool = jax.random.normal(ks[3], (DEPTH, DEC_BATCH, POOL_MAX - 1, D_POOL), f32)
    norm_g = 1.0 + 0.02 * jax.random.normal(ks[4], (DEPTH, D_MODEL), f32)
    w_in = jax.random.normal(ks[5], (DEPTH, D_MODEL, D_IN), f32) * D_MODEL ** -0.5
    conv_w = jax.random.normal(ks[6], (DEPTH, CONV_W, D_CONV), f32) * CONV_W ** -0.5
    pool_w = jax.random.normal(ks[7], (DEPTH, N_POOL_GROUPS, POOL_GC, POOL_GC), f32) * POOL_GC ** -0.5
    pool_scale = 1.0 + 0.02 * jax.random.normal(ks[8], (DEPTH, D_POOL), f32)
    w_out = jax.random.normal(ks[9], (DEPTH, D_MIX, D_MODEL), f32) * D_MIX ** -0.5
    final_g = 1.0 + 0.02 * jax.random.normal(ks[10], (D_MODEL,), f32)
    return {"x_prompt": x_prompt, "x_sample": x_sample,
            "state_conv": state_conv, "state_pool": state_pool,
            "norm_g": norm_g, "w_in": w_in, "conv_w": conv_w, "pool_w": pool_w,
            "pool_scale": pool_scale, "w_out": w_out, "final_g": final_g}


def reference(x_prompt, x_sample, state_conv, state_pool, norm_g, w_in, conv_w, pool_w,
              pool_scale, w_out, final_g):
    hp, hs = x_prompt, x_sample
    zero_conv = jnp.zeros((BATCH, CONV_W - 1, D_CONV), x_prompt.dtype)
    zero_pool = jnp.zeros((BATCH, POOL_MAX - 1, D_POOL), x_prompt.dtype)
    conv_p, pool_p, conv_s, pool_s = [], [], [], []
    for l in range(DEPTH):
        yp, ncp, npp = _mixer(_rmsnorm(hp, norm_g[l]), zero_conv, zero_pool, 0,
                              w_in[l], conv_w[l], pool_w[l], pool_scale[l], w_out[l])
        ys, ncs, nps = _mixer(_rmsnorm(hs, norm_g[l]), state_conv[l], state_pool[l], PAST_LEN,
                              w_in[l], conv_w[l], pool_w[l], pool_scale[l], w_out[l])
        hp = hp + yp
        hs = hs + ys
        conv_p.append(ncp); pool_p.append(npp); conv_s.append(ncs); pool_s.append(nps)
    y_prompt = _rmsnorm(hp, final_g)
    y_sample = _rmsnorm(hs, final_g)
    new_state_conv_p = jnp.stack(conv_p, axis=0)
    new_state_pool_p = jnp.stack(pool_p, axis=0)
    new_state_conv_s = jnp.stack(conv_s, axis=0)
    new_state_pool_s = jnp.stack(pool_s, axis=0)
    return (y_prompt, y_sample, new_state_conv_p, new_state_pool_p, new_state_conv_s, new_state_pool_s)
```

```python
import contextlib
import os
import numpy as np
import concourse.bass as bass
import concourse.mybir as mybir
from concourse.bass_utils import run_bass_kernel_spmd

F32 = mybir.dt.float32
BF16 = mybir.dt.bfloat16
AF = mybir.ActivationFunctionType
ALU = mybir.AluOpType

N_CORES = 8
D = 1024
DIN = 3072
NKC = 8
SEQ = 2048
NS = 16
TPOS = 4
EPS = 1e-6
COL_B, COL_C, COL_U, COL_ZA, COL_V, COL_ZB = 0, 512, 1024, 1536, 2048, 2560
WINDOWS = (2, 4, 8, 16)
ANNOTATE = False
STRICT = bool(os.environ.get("KSTRICT"))

W_PIECES = [[(COL_V + 128 * g, 128)] for g in (3, 2, 1, 0)] + [[(COL_ZB, 512)]] + [
    [(COL_U + 128 * j, 128), (COL_C + 128 * j, 128), (COL_ZA + 128 * j, 128), (COL_B + 128 * j, 128)]
    for j in range(4)]


def _w_in_layout():
    off = 0
    table = {}
    piece_rng = []
    for k, subs in enumerate(W_PIECES):
        p0 = off
        for (c0, n) in subs:
            for c in range(c0, c0 + n, 128):
                table[c] = (k, off + (c - c0), n)
            off += NKC * n
        piece_rng.append((p0, off))
    return table, piece_rng, off


W_TABLE, W_RNG, W_TOT = _w_in_layout()


class Sched:
    ENGS = ("pe", "act", "dve", "pool", "sp")

    def __init__(self):
        self.ops = {e: [] for e in self.ENGS}
        self.count = {}
        self.inc = {}
        self.tag = "init"
        self.last = {}
        for e in ("pe", "act", "dve", "pool"):
            self.new_sem(e, 1)

    def new_sem(self, name, inc):
        self.count[name] = 0
        self.inc[name] = inc

    def op(self, eng, fn, waits=(), sem="default"):
        if sem == "default":
            sem = eng if eng != "sp" else None
        t = None
        if sem is not None:
            self.count[sem] += self.inc[sem]
            t = (sem, self.count[sem])
        ws = []
        if STRICT and eng in ("act", "dve", "pool") and self.last.get(eng) is not None:
            ws.append(self.last[eng])

        def flat(w):
            if w is None:
                return
            if isinstance(w, list):
                for x in w:
                    flat(x)
            else:
                ws.append(w)
        for w in waits:
            flat(w)
        self.ops[eng].append((fn, ws, t, self.tag))
        if t is not None and t[0] == eng:
            self.last[eng] = t
        return t


def build_program():
    nc = bass.Bass("TRN2", target_bir_lowering=False)
    S = Sched()

    def din(name, shape):
        return nc.dram_tensor(name, list(shape), F32, kind="ExternalInput")

    def dout(name, shape):
        return nc.dram_tensor(name, list(shape), F32, kind="ExternalOutput")

    xp_h = din("xp", [SEQ, D]); xs_h = din("xs", [NS * TPOS, D])
    sc_h = din("sc", [2 * NS, 512]); sp_h = din("sp", [15 * NS, 512])
    gc_h = din("gcols", [128, 16]); win_h = din("w_in", [128, W_TOT]); cw_h = din("cw", [128, 12])
    pw_h = din("pw", [4, 128, 128]); psc_h = din("psc", [128, 4]); wout_h = din("w_out", [128, NKC * D])
    yp_h = dout("yp", [SEQ, D]); ys_h = dout("ys", [NS * TPOS, D])
    ncp_h = dout("ncp", [2, 512]); npp_h = dout("npp", [15, 512])
    ncs_h = dout("ncs", [2 * NS, 512]); nps_h = dout("nps", [15 * NS, 512])
    xp, xs_d, sc_d, sp_d = xp_h.ap(), xs_h.ap(), sc_h.ap(), sp_h.ap()
    yp, ys_d = yp_h.ap(), ys_h.ap()
    win_d = win_h.ap()
    wout_d = wout_h.ap()
    pw_v = pw_h.ap().rearrange("g c d -> c g d")

    es = contextlib.ExitStack()
    with es:
        def sb(name, shape, dt=F32):
            return es.enter_context(nc.sbuf_tensor(name, list(shape), dt))

        def ps(name, shape, dt=F32):
            return es.enter_context(nc.psum_tensor(name, list(shape), dt))

        w_in = sb("w_in_bf", [128, W_TOT], BF16)
        w_out = sb("w_out_bf", [128, 2, NKC, 512], BF16)
        pw = sb("pw_bf", [128, 4, 128], BF16)
        gb = sb("gb", [128, D]); fgb = sb("fgb", [128, D])
        gcols = sb("gcols_sb", [128, 16])
        cw = sb("cw_sb", [128, 12]); psc = sb("psc_sb", [128, 4])
        ident_bf = sb("ident_bf", [128, 128], BF16); ident_f = sb("ident_f", [128, 128])
        invcnt = sb("invcnt", [128, 16]); neghalf = sb("neghalf", [128, 1])
        tmpfix = sb("tmpfix", [128, 16])
        NXS = 8
        xt = [sb(f"xt{i}", [128, D]) for i in range(NXS)]
        xt_s = sb("xt_s", [128, D])
        xsb = [sb(f"xsb{i}", [128, D], BF16) for i in range(2)]
        hT = [sb(f"hT{i}", [128, NKC, 512], BF16) for i in range(2)]
        hT_s = sb("hT_s", [128, NKC, 64], BF16)
        NSUB = 17
        ssx = sb("ssx", [128, NSUB]); rx = sb("rx", [128, NSUB])
        ssh = sb("ssh", [128, NSUB]); rh = sb("rh", [128, NSUB])
        sa = [sb(f"sa{i}", [128, 512]) for i in range(2)]
        Ab = [sb(f"A{i}", [128, 512]) for i in range(2)]
        sa_s = sb("sa_s", [128, 64]); Ab_s = sb("Ab_s", [128, 64])
        cu_ext = [sb(f"cu_ext{j}", [128, 514]) for j in range(4)]
        cu_ext_s = sb("cu_ext_s", [128, 4, 96])
        v_ext = [sb(f"v_ext{g}", [128, 527]) for g in range(4)]
        v_ext_s = sb("v_ext_s", [128, 4, 304])
        sbz = [sb(f"sbz{g}", [128, 512]) for g in range(4)]
        sbz_s = sb("sbz_s", [128, 4, 64])
        wsb = {2: sb("s2buf", [128, 527]), 4: sb("s4buf", [128, 527]),
               8: sb("s8buf", [128, 527]), 16: sb("s16buf", [128, 527])}
        pooled = [sb(f"pooled{g}", [128, 512], BF16) for g in range(4)]
        pooledS = [sb(f"pooledS{g}", [128, 64], BF16) for g in range(4)]
        mix = sb("mix", [128, NKC, 512], BF16)
        mix_s = sb("mix_s", [128, NKC, 64], BF16)
        NH = 3
        hsb = [sb(f"hsb{i}", [128, D]) for i in range(NH)]
        junk = sb("junk", [128, D], BF16)

        pT = ps("pT", [128, 1024], BF16)
        pP = [ps(f"pP{i}", [128, 512]) for i in range(4)]
        pW = ps("pW", [128, 512])
        pO = [ps("pO0", [128, 512]), ps("pO1", [128, 512])]
        pT3 = pT[:].rearrange("p (k t) -> p k t", k=NKC)
        pO0b3 = pO[0].bitcast(BF16)[:].rearrange("p (k t) -> p k t", k=NKC)

        for i in range(NXS):
            S.new_sem(f"x{i}", 16)
        S.new_sem("xs16", 16)
        for i in range(NH):
            S.new_sem(f"y{i}", 16)
        S.new_sem("const", 16); S.new_sem("stin", 16); S.new_sem("stout", 16); S.new_sem("gbs", 16)
        S.new_sem("pwd", 16)

        def ACT(out, in_, func, waits, scale=None, accum=None):
            kw = {}
            if scale is not None:
                kw["scale"] = scale
            if accum is not None:
                kw["accum_out"] = accum
            return S.op("act", lambda e: e.activation(out=out, in_=in_, func=func, **kw), waits=waits)

        def TT(eng, out, in0, in1, op, waits):
            return S.op(eng, lambda e: e.tensor_tensor(out=out, in0=in0, in1=in1, op=op), waits=waits)

        def STT(out, in0, scalar, in1, op0, op1, waits):
            return S.op("dve", lambda e: e.scalar_tensor_tensor(out=out, in0=in0, scalar=scalar, in1=in1,
                                                                op0=op0, op1=op1), waits=waits)

        def TS(eng, out, in0, s1, s2, op0, op1, waits):
            return S.op(eng, lambda e: e.tensor_scalar(out=out, in0=in0, scalar1=s1, scalar2=s2, op0=op0, op1=op1),
                        waits=waits)

        def CP(eng, out, in_, waits):
            return S.op(eng, lambda e: e.tensor_copy(out=out, in_=in_), waits=waits)

        def MSET(eng, ap, val, waits=()):
            return S.op(eng, lambda e: e.memset(ap, val), waits=waits)

        def MM(out, lhsT, rhs, start, stop, waits=(), inc=False):
            return S.op("pe", lambda e: e.matmul(out, lhsT=lhsT, rhs=rhs, start=start, stop=stop),
                        waits=waits, sem=("pe" if inc else None))

        def TR(out, in_, ident, waits=(), inc=False):
            return S.op("pe", lambda e: e.transpose(out=out, in_=in_, identity=ident),
                        waits=waits, sem=("pe" if inc else None))

        def DMA(eng, out, in_, waits, sem):
            return S.op(eng, lambda e: e.dma_start(out=out, in_=in_), waits=waits, sem=sem)

        t_m1 = MSET("pool", ident_bf[:], 0.0)
        t_m2 = MSET("pool", ident_f[:], 0.0)
        t_nh0 = MSET("pool", neghalf[:], -0.5)
        t_identb = S.op("pool", lambda e: e.affine_select(
            out=ident_bf[:], in_=ident_bf[:], compare_op=ALU.not_equal, fill=1.0, base=0,
            pattern=[[-1, 128]], channel_multiplier=1), waits=[t_m1])
        t_identf = S.op("pool", lambda e: e.affine_select(
            out=ident_f[:], in_=ident_f[:], compare_op=ALU.not_equal, fill=1.0, base=0,
            pattern=[[-1, 128]], channel_multiplier=1), waits=[t_m2, t_nh0])

        junk_t = [ACT(junk[:, 0:1], neghalf[:, 0:1], AF.Silu, [t_nh0])]

        t_halo = None
        for g in range(4):
            MSET("dve", v_ext[g][:, 0:15], 0.0)
            MSET("dve", cu_ext[g][:, 0:2], 0.0)
        for t in range(16):
            t_halo = MSET("dve", invcnt[:, t:t + 1], 1.0 / (t + 1))
        t_dve_init = t_halo

        wt = {}
        wo_t = {}

        def issue_wpiece(k, waits=()):
            a, b_ = W_RNG[k]
            name = f"w{k}"
            S.new_sem(name, 16)
            wt[k] = DMA("pool", w_in[:, a:b_], win_d[:, a:b_], list(waits), name)

        def issue_wout(h):
            name = f"wo{h}"
            S.new_sem(name, 16)
            wo_t[h] = DMA("pool", w_out[:, h, :, :].rearrange("p k c -> p (k c)"),
                          wout_d[:, h * NKC * 512:(h + 1) * NKC * 512], [], name)

        def w_lhsT(ecol, kc):
            k, off, n = W_TABLE[ecol]
            return w_in[:, off + kc * n: off + kc * n + 128], wt[k]

        def xrows(gs):
            if gs < 16:
                return xp[gs * 128:(gs + 1) * 128, :], 128
            return xs_d[:, :], 64

        def yrows(gs):
            if gs < 16:
                return yp[gs * 128:(gs + 1) * 128, :], 128
            return ys_d[:, :], 64

        def xtile(gs):
            return xt[gs % NXS] if gs < 16 else xt_s

        x_t = {}
        h_t = {}

        def load_x(gs, extra=(), eng="sp"):
            src, m = xrows(gs)
            if gs == 16:
                x_t[gs] = DMA("sp", xt_s[0:m, :], src, list(extra), "xs16")
                return
            slot = gs % NXS
            x_t[gs] = DMA("sp", xt[slot][0:m, :], src, [h_t.get(gs - NXS)] + list(extra), f"x{slot}")

        t_gc = DMA("sp", gcols[:], gc_h.ap(), [], "gbs")
        for gs in range(4):
            load_x(gs)
        load_x(16)
        DMA("sp", cw[:], cw_h.ap(), [], "const")
        t_const = DMA("sp", psc[:], psc_h.ap(), [], "const")
        DMA("sp", hsb[0][:, 0:512], sp_d[0:128, :], [], "stin")
        DMA("sp", hsb[0][0:112, 512:1024], sp_d[128:240, :], [], "stin")
        t_stin = DMA("sp", hsb[1][0:32, 0:512], sc_d[:, :], [], "stin")

        t_pw_box = [None]

        bankP_free = [None] * 4
        bankW_free = [None]
        bankO_free = [None, None]
        bankT_free = [None]
        xsb_free = [None, None]
        hsb_free = [None] * NH
        sbz_free = {}
        saA_free = {}
        cu_roll = [t_dve_init] * 4
        v_roll = [t_dve_init] * 4
        wsb_free = {2: None, 4: None, 8: None, 16: None}
        wsb_pool_rw = {}
        fix_t = [None]
        ss_t = {}; r_t = {}; xs_t = {}; tr_t = {}; ev_t = {}
        chunk_ctr = [0]
        conv_ctr = [0]
        xs_ctr = [0]
        h_ctr = [0]
        xs_q = {}

        S.tag = "gains"
        t_gb = []
        t_fgb = []
        for which, dst, banks in ((0, gb, (pO[0], pO[1])), (1, fgb, (pP[0], pP[1]))):
            for half in range(2):
                t = None
                for k4 in range(4):
                    kc = half * 4 + k4
                    src = bass.AP(gcols, which * 8 + kc, [[16, 128], [0, 128]])
                    t = TR(banks[half][:, k4 * 128:(k4 + 1) * 128], src, ident_f[:, :],
                           waits=([t_gc, t_identf] if k4 == 0 else ()), inc=(k4 == 3))
                cs = slice(half * 512, (half + 1) * 512)
                if which == 0:
                    tc = CP("dve", dst[:, cs], banks[half][:, :], [t])
                    bankO_free[half] = tc
                    t_gb.append(tc)
                else:
                    tc = CP("dve", dst[:, cs], banks[half][:, :], [t])
                    bankP_free[half] = tc
                    t_fgb.append(tc)

        def in_sumsq(gs):
            if gs in ss_t:
                return
            _, m = xrows(gs)
            ss_t[gs] = ACT(junk[0:m, :], xtile(gs)[0:m, :], AF.Square, [x_t[gs], junk_t[0]], accum=ssx[0:m, gs:gs + 1])
            junk_t[0] = ss_t[gs]

        def in_r(gs):
            _, m = xrows(gs)
            t1 = TS("pool", rx[0:m, gs:gs + 1], ssx[0:m, gs:gs + 1], 1.0 / D, EPS, ALU.mult, ALU.add, [ss_t[gs]])
            r_t[gs] = TT("pool", rx[0:m, gs:gs + 1], rx[0:m, gs:gs + 1], neghalf[0:m, :], ALU.pow, [t1])

        def in_xs(gs, on_pool=False):
            _, m = xrows(gs)
            q = xs_ctr[0] % 2
            xs_ctr[0] += 1
            xs_q[gs] = q
            if on_pool:
                tl = None
                for hf in range(4):
                    cs = slice(hf * 256, (hf + 1) * 256)
                    t1 = TS("pool", hsb[2][0:m, cs], xtile(gs)[0:m, cs], rx[0:m, gs:gs + 1], 1.0, ALU.mult, ALU.mult,
                            [r_t[gs], hsb_free[2]])
                    tl = TT("pool", xsb[q][0:m, cs], hsb[2][0:m, cs], gb[0:m, cs], ALU.mult, [t1, t_gb, xsb_free[q]])
                xs_t[gs] = tl
                hsb_free[2] = tl
                return
            xs_t[gs] = STT(xsb[q][0:m, :], xtile(gs)[0:m, :], rx[0:m, gs:gs + 1], gb[0:m, :], ALU.mult, ALU.mult,
                           [r_t[gs], t_gb, xsb_free[q]])

        def in_transposes(gs, alt=False):
            _, m = xrows(gs)
            q = xs_q[gs]
            bank3 = pO0b3 if alt else pT3
            free = (bankO_free[0] if alt else bankT_free[0])
            t = None
            for kc in range(NKC):
                t = TR(bank3[:, kc, 0:m], xsb[q][0:m, kc * 128:(kc + 1) * 128], ident_bf[0:m, 0:m],
                       waits=([xs_t[gs], free, t_identb] if kc == 0 else ()), inc=(kc == NKC - 1))
            tr_t[gs] = t
            xsb_free[q] = t

        def in_evac(gs, dst3, sub, alt=False):
            _, m = xrows(gs)
            c0 = sub * 128
            dst = dst3[:, :, c0:c0 + m]
            if alt:
                ev_t[gs] = CP("dve", dst, pO0b3[:, :, 0:m], [tr_t[gs]])
                bankO_free[0] = ev_t[gs]
            else:
                ev_t[gs] = ACT(dst, pT3[:, :, 0:m], AF.Copy, [tr_t[gs]])
                bankT_free[0] = ev_t[gs]

        def next_in(NB, idx, on_pool=False):
            if NB is None or idx >= len(NB.subs):
                return
            gs, _ = NB.subs[idx]
            in_sumsq(gs)
            in_r(gs)
            in_xs(gs, on_pool=on_pool)

        class Blk:
            pass

        def mk(kind, b, bi, ntok, U, T, subs, hTb, mixb):
            B = Blk()
            B.kind = kind; B.b = b; B.bi = bi; B.ntok = ntok; B.U = U; B.T = T; B.subs = subs
            B.is_p = kind == "p"; B.hT = hTb; B.mix = mixb
            B.hw_c = 2 * U; B.hw_v = 15 * U
            B.pooled_t = {}; B.silu_b_t = {}; B.ya_t = {}; B.yb_t = {}
            B.pooled = pooled if B.is_p else pooledS
            return B

        Bs = [mk("p", b, b, 512, 1, 512, [(4 * b + s, s) for s in range(4)], hT[b % 2], mix) for b in range(4)]
        BS = mk("s", 4, 4, 64, 16, 4, [(16, 0)], hT_s, mix_s)

        def cuE(B, j, a, b_):
            return cu_ext[j][:, a:b_] if B.is_p else cu_ext_s[:, j, a:b_]

        def vE(B, g, a, b_):
            return v_ext[g][:, a:b_] if B.is_p else v_ext_s[:, g, a:b_]

        def sbzE(B, g):
            return sbz[g][:, 0:B.ntok] if B.is_p else sbz_s[:, g, :]

        t_state = []

        def state_in_piece(i):
            S.tag = "state_in"
            bank = pO[(i + 1) % 2]
            bi_ = (i + 1) % 2
            if i < 4:
                g = i
                TR(bank[:, 0:128], hsb[0][:, g * 128:(g + 1) * 128], ident_f[:, :],
                   waits=[t_stin, t_identf, bankO_free[bi_]])
                t = TR(bank[:, 128:240], hsb[0][0:112, 512 + g * 128:512 + (g + 1) * 128], ident_f[0:112, 0:112], inc=True)
                tc = ACT(v_ext_s[:, g, 0:240], bank[:, 0:240], AF.Copy, [t])
                if i == 3:
                    hsb_free[0] = tc
            else:
                t = None
                for j in range(4):
                    t = TR(bank[:, j * 32:(j + 1) * 32], hsb[1][0:32, j * 128:(j + 1) * 128], ident_f[0:32, 0:32],
                           waits=([t_stin, t_identf, bankO_free[bi_]] if j == 0 else ()), inc=(j == 3))
                tc = ACT(cu_ext_s[:, :, 0:32], bank[:, 0:128].rearrange("p (j c) -> p j c", j=4), AF.Copy, [t])
                hsb_free[1] = tc
            bankO_free[bi_] = tc
            t_state.append(tc)

        S.tag = "b0.in"
        first = [gs for gs, _ in Bs[0].subs] + [16]
        for gs in first:
            in_sumsq(gs)
        for i, gs in enumerate(first):
            alt = (i % 2 == 1)
            if i == 2:
                issue_wpiece(0, waits=[x_t[2]])
            in_r(gs)
            if i == 3:
                for k in range(1, 4):
                    issue_wpiece(k)
                t_pw_box[0] = DMA("pool", pw[:], pw_v, [], "pwd")
            in_xs(gs)
            if gs < 16:
                in_transposes(gs, alt=alt)
                in_evac(gs, Bs[0].hT, gs % 4, alt=alt)
        S.tag = "wdma"
        issue_wpiece(4)
        issue_wpiece(5)
        for gs in range(4, 8):
            load_x(gs, extra=[wt[4] if gs < 6 else wt[5]])

        store_t = {}

        def proj_chunk(B, ecol, extra_waits=()):
            i = chunk_ctr[0] % 4
            chunk_ctr[0] += 1
            ntok = B.ntok
            t = None
            for kc in range(NKC):
                lhsT, wtick = w_lhsT(ecol, kc)
                t = MM(pP[i][:, 0:ntok], lhsT, B.hT[:, kc, 0:ntok],
                       start=(kc == 0), stop=(kc == NKC - 1),
                       waits=([bankP_free[i], wtick] + list(extra_waits) if kc == 0 else ()),
                       inc=(kc == NKC - 1))
            return i, t

        def emit_pool_v(B, g, evw):
            ntok = B.ntok
            S.tag = f"b{B.bi}.v{g}"
            i, tp = proj_chunk(B, COL_V + 128 * g, extra_waits=evw)
            tv = ACT(vE(B, g, B.hw_v, B.hw_v + ntok), pP[i][:, 0:ntok], AF.Copy,
                     [tp, v_roll[g] if B.is_p else None])
            bankP_free[i] = tv
            B.tv = getattr(B, "tv", {})
            B.tv[g] = tv

        def emit_pool_sums(B, g, pooled_now=True):
            ntok, U, T = B.ntok, B.U, B.T
            S.tag = f"b{B.bi}.v{g}"
            w = WINDOWS[g]
            need = {w: 15}
            ww = w
            while ww > 2:
                need[ww // 2] = need[ww] - ww // 2
                ww //= 2
            prev_t = [B.tv[g]] + ([] if B.is_p else list(t_state))
            for ww in sorted(need):
                lo = need[ww]; hi = 15 + T
                sh = ww // 2
                if ww == 2:
                    in0 = vE(B, g, lo * U, hi * U); in1 = vE(B, g, (lo - 1) * U, (hi - 1) * U)
                else:
                    src = wsb[ww // 2]
                    in0 = src[:, lo * U:hi * U]; in1 = src[:, (lo - sh) * U:(hi - sh) * U]
                prev_t = TT("pool", wsb[ww][:, lo * U:hi * U], in0, in1, ALU.add,
                            [prev_t, wsb_free[ww], wsb_pool_rw.get(ww)])
                wsb_pool_rw[ww] = prev_t
                if ww > 2:
                    wsb_pool_rw[ww // 2] = prev_t
            fin = wsb[w]
            if not pooled_now:
                B.wsum_t = getattr(B, "wsum_t", {})
                B.wsum_t[g] = prev_t
                return
            tpo = STT(B.pooled[g][:, 0:ntok], fin[:, 15 * U:(15 + T) * U], 1.0 / w, vE(B, g, B.hw_v, B.hw_v + ntok),
                      ALU.mult, ALU.subtract, [prev_t])
            if B.is_p and B.b == 0:
                nfx = w - 1
                tf1 = TT("dve", tmpfix[:, 0:nfx], fin[:, 15:15 + nfx], invcnt[:, 0:nfx], ALU.mult, [tpo, fix_t[0]])
                tpo = TT("dve", pooled[g][:, 0:nfx], tmpfix[:, 0:nfx], v_ext[g][:, 15:15 + nfx], ALU.subtract, [tf1])
                fix_t[0] = tpo
            B.pooled_t[g] = tpo
            wsb_free[w] = tpo
            if B.is_p and B.b < 3:
                v_roll[g] = CP("dve", v_ext[g][:, 0:15], v_ext[g][:, 512:527], [tpo])

        def emit_zb(B, g):
            S.tag = f"b{B.bi}.zB{g}"
            i, tp = proj_chunk(B, COL_ZB + 128 * g)
            ts_ = ACT(sbzE(B, g), pP[i][:, 0:B.ntok], AF.Silu, [tp, sbz_free.get((B.is_p, g))])
            bankP_free[i] = ts_
            B.silu_b_t[g] = ts_

        def emit_poolw(B, g):
            ntok = B.ntok
            S.tag = f"b{B.bi}.poolw{g}"
            tpw = MM(pW[:, 0:ntok], pw[:, g, :], B.pooled[g][:, 0:ntok], True, True,
                     waits=[B.pooled_t[g], bankW_free[0], t_pw_box[0]], inc=True)
            tyb = STT(B.mix[:, 4 + g, 0:ntok], pW[:, 0:ntok], psc[:, g:g + 1], sbzE(B, g),
                      ALU.mult, ALU.mult, [tpw, B.silu_b_t[g], t_const])
            bankW_free[0] = tyb
            sbz_free[(B.is_p, g)] = tyb
            B.yb_t[g] = tyb

        def emit_sample_pooled(B):
            ntok, U, T = B.ntok, B.U, B.T
            S.tag = "b4.pooled"
            for g in (3, 2, 1, 0):
                w = WINDOWS[g]
                tpo = STT(B.pooled[g][:, 0:ntok], wsb[w][:, 15 * U:(15 + T) * U], 1.0 / w,
                          vE(B, g, B.hw_v, B.hw_v + ntok), ALU.mult, ALU.subtract, [B.wsum_t[g]])
                B.pooled_t[g] = tpo
                wsb_free[w] = tpo

        def emit_sample_pool_tail(B):
            ntok = B.ntok
            S.tag = "b4.pooltail"
            tp = None
            for g in (3, 2, 1, 0):
                tp = MM(pW[:, g * 64:(g + 1) * 64], pw[:, g, :], B.pooled[g][:, 0:ntok], True, True,
                        waits=([bankW_free[0], t_pw_box[0]] + [B.pooled_t[q] for q in range(4)] if g == 3 else ()),
                        inc=(g == 0))
            tyb = None
            for g in (3, 2, 1, 0):
                tyb = STT(B.mix[:, 4 + g, 0:ntok], pW[:, g * 64:(g + 1) * 64], psc[:, g:g + 1], sbzE(B, g),
                          ALU.mult, ALU.mult, [tp, B.silu_b_t[g], t_const])
                B.yb_t[g] = tyb
                sbz_free[(B.is_p, g)] = tyb
            bankW_free[0] = tyb

        class ConvState:
            pass

        def conv_bufs(B):
            if B.is_p:
                k = conv_ctr[0] % 2
                conv_ctr[0] += 1
                return sa[k][:, 0:B.ntok], Ab[k][:, 0:B.ntok], ("p", k)
            return sa_s[:, :], Ab_s[:, :], ("s", 0)

        def conv_u(B, j, C):
            S.tag = f"b{B.bi}.conv{j}"
            C.sa, C.A, C.key = conv_bufs(B)
            C.body = cuE(B, j, B.hw_c, B.hw_c + B.ntok)
            iu, tpu = proj_chunk(B, COL_U + 128 * j)
            C.tu = ACT(C.body, pP[iu][:, 0:B.ntok], AF.Copy, [tpu, cu_roll[j] if B.is_p else None])
            bankP_free[iu] = C.tu

        def conv_taps12(B, j, C, tt1):
            ntok, U = B.ntok, B.U
            tt2 = STT(C.A, cuE(B, j, U, U + ntok), cw[:, 3 * j + 1:3 * j + 2], C.A, ALU.mult, ALU.add,
                      [tt1, C.tcu] + ([] if B.is_p else list(t_state)))
            C.tyc = STT(C.A, cuE(B, j, 2 * U, 2 * U + ntok), cw[:, 3 * j + 2:3 * j + 3], C.A, ALU.mult, ALU.add, [tt2])
            if B.is_p and B.b < 3:
                cu_roll[j] = CP("dve", cu_ext[j][:, 0:2], cu_ext[j][:, 512:514], [C.tyc])

        def conv_c(B, j, C):
            S.tag = f"b{B.bi}.conv{j}"
            ntok = B.ntok
            ic, tpc = proj_chunk(B, COL_C + 128 * j)
            C.tcu = TT("dve", C.body, pP[ic][:, 0:ntok], C.body, ALU.mult, [tpc, C.tu])
            bankP_free[ic] = C.tcu
            C.tt1 = None
            if j == 3 and B.is_p and B.bi > 0:
                C.tt1 = TS("pool", C.A, cuE(B, j, 0, ntok), cw[:, 3 * j:3 * j + 1], 1.0, ALU.mult, ALU.mult,
                           [C.tcu, saA_free.get(C.key), t_const])

        def conv_za(B, j, C):
            S.tag = f"b{B.bi}.conv{j}"
            iz, tpz = proj_chunk(B, COL_ZA + 128 * j)
            C.tsl = ACT(C.sa, pP[iz][:, 0:B.ntok], AF.Silu, [tpz, saA_free.get(C.key)])
            bankP_free[iz] = C.tsl
            if C.tt1 is not None:
                conv_taps12(B, j, C, C.tt1)

        def conv_b(B, j, C):
            S.tag = f"b{B.bi}.conv{j}"
            ntok = B.ntok
            ib, tpb = proj_chunk(B, COL_B + 128 * j)
            tgt = TT("dve", C.sa, pP[ib][:, 0:ntok], C.sa, ALU.mult, [tpb, C.tsl])
            bankP_free[ib] = tgt
            if C.tt1 is None:
                tt1 = ACT(C.A, cuE(B, j, 0, ntok), AF.Copy,
                          [C.tcu, saA_free.get(C.key), t_const] + ([] if B.is_p else list(t_state)),
                          scale=cw[:, 3 * j:3 * j + 1])
                conv_taps12(B, j, C, tt1)
            tya = TT("dve", B.mix[:, j, 0:ntok], C.A, C.sa, ALU.mult, [C.tyc, tgt])
            B.ya_t[j] = tya
            saA_free[C.key] = tya

        def emit_out_sub(B, gs, sub, first_sub=False):
            _, m = xrows(gs)
            c0 = sub * 128
            hk = h_ctr[0] % NH
            h_ctr[0] += 1
            xtl = xtile(gs)
            S.tag = f"b{B.bi}.out{sub}"
            korder = [7, 6, 5, 4, 0, 1, 2, 3]
            kwait = {4: B.yb_t[0], 5: B.yb_t[1], 6: B.yb_t[2], 7: B.yb_t[3],
                     0: B.ya_t[0], 1: B.ya_t[1], 2: B.ya_t[2], 3: B.ya_t[3]}
            tl = [None, None]
            if first_sub:
                seq = [(half, idx, kc) for half in range(2) for idx, kc in enumerate(korder[:-1])]
                seq += [(half, NKC - 1, korder[-1]) for half in range(2)]
            else:
                seq = [(half, idx, kc) for half in range(2) for idx, kc in enumerate(korder)]
            for (half, idx, kc) in seq:
                w = [kwait[kc]]
                if idx == 0:
                    w += [bankO_free[half], wo_t[half]]
                tl[half] = MM(pO[half][0:m, :], B.mix[:, kc, c0:c0 + m], w_out[:, half, kc, :],
                              (idx == 0), (idx == NKC - 1), waits=w, inc=(idx == NKC - 1))
            ths = []
            for half in range(2):
                th = TT("dve", hsb[hk][0:m, half * 512:(half + 1) * 512], pO[half][0:m, :],
                        xtl[0:m, half * 512:(half + 1) * 512], ALU.add, [tl[half], hsb_free[hk]])
                bankO_free[half] = th
                ths.append(th)
            h_t[gs] = ths[1]
            tsq = ACT(junk[0:m, :], hsb[hk][0:m, :], AF.Square, ths + [junk_t[0]], accum=ssh[0:m, gs:gs + 1])
            junk_t[0] = tsq
            t1 = TS("pool", rh[0:m, gs:gs + 1], ssh[0:m, gs:gs + 1], 1.0 / D, EPS, ALU.mult, ALU.add, [tsq])
            tr2 = TT("pool", rh[0:m, gs:gs + 1], rh[0:m, gs:gs + 1], neghalf[0:m, :], ALU.pow, [t1])

            def emit_yout():
                S.tag = f"b{B.bi}.yout{sub}"
                ty = STT(hsb[hk][0:m, :], hsb[hk][0:m, :], rh[0:m, gs:gs + 1], fgb[0:m, :], ALU.mult, ALU.mult,
                         [tr2, tsq, t_fgb])
                dst, _ = yrows(gs)
                tst = DMA("sp", dst, hsb[hk][0:m, :], [ty], f"y{hk}")
                hsb_free[hk] = tst
                store_t[gs] = tst
                nxt_gs = gs + NXS
                if nxt_gs < 16:
                    load_x(nxt_gs)
            return emit_yout

        def emit_state_out_prompt(B):
            S.tag = "p.state_out"
            t = None
            for g in range(4):
                t = TR(pP[0][0:15, g * 128:(g + 1) * 128], v_ext[g][:, 512:527], ident_f[:, :],
                       waits=([bankP_free[0]] + [B.pooled_t[q] for q in range(4)] if g == 0 else ()), inc=(g == 3))
            tc = ACT(xt[1][0:15, 0:512], pP[0][0:15, :], AF.Copy, [t, h_t[9]])
            bankP_free[0] = tc
            DMA("sp", npp_h.ap(), xt[1][0:15, 0:512], [tc], "stout")
            for j in range(4):
                t = TR(pP[1][0:2, j * 128:(j + 1) * 128], cu_ext[j][:, 512:514], ident_f[:, :],
                       waits=([bankP_free[1]] + [B.ya_t[q] for q in range(4)] if j == 0 else ()), inc=(j == 3))
            tc = ACT(xt[1][0:2, 512:1024], pP[1][0:2, :], AF.Copy, [t, h_t[9]])
            bankP_free[1] = tc
            DMA("sp", ncp_h.ap(), xt[1][0:2, 512:1024], [tc], "stout")

        def emit_state_out_sample(B, part):
            S.tag = "s.state_out"
            t = None
            if part == 0:
                for j in range(4):
                    t = TR(pP[2][0:32, j * 128:(j + 1) * 128], cu_ext_s[:, j, 64:96], ident_f[:, :],
                           waits=([bankP_free[2]] + [B.ya_t[q] for q in range(4)] if j == 0 else ()), inc=(j == 3))
                tc = ACT(xt[2][0:32, 0:512], pP[2][0:32, :], AF.Copy, [t, h_t[10]])
                bankP_free[2] = tc
                DMA("sp", ncs_h.ap(), xt[2][0:32, 0:512], [tc], "stout")
                for g in range(4):
                    t = TR(pP[3][:, g * 128:(g + 1) * 128], v_ext_s[:, g, 64:192], ident_f[:, :],
                           waits=([bankP_free[3]] + [B.pooled_t[q] for q in range(4)] if g == 0 else ()), inc=(g == 3))
                tc = ACT(xt[3][:, 0:512], pP[3][:, :], AF.Copy, [t, h_t[11]])
                bankP_free[3] = tc
                DMA("sp", nps_h.ap()[0:128, :], xt[3][:, 0:512], [tc], "stout")
            else:
                for g in range(4):
                    t = TR(pP[2][0:112, g * 128:(g + 1) * 128], v_ext_s[:, g, 192:304], ident_f[:, :],
                           waits=([bankP_free[2]] if g == 0 else ()), inc=(g == 3))
                tc = ACT(xt[2][0:112, 512:1024], pP[2][0:112, :], AF.Copy, [t, h_t[10]])
                bankP_free[2] = tc
                DMA("sp", nps_h.ap()[128:240, :], xt[2][0:112, 512:1024], [tc], "stout")

        for bi in range(4):
            B = Bs[bi]
            NB = Bs[bi + 1] if bi + 1 < 4 else None
            group = [B, BS] if bi == 0 else [B]
            evw = {id(X): [ev_t.get(gs) for gs, _ in X.subs] for X in group}

            for g in (3, 2, 1, 0):
                if bi > 0 and NB is not None:
                    k = {3: 0, 1: 1}.get(g)
                    if k is not None and k < len(NB.subs):
                        S.tag = f"b{bi}.nxt_in"
                        in_sumsq(NB.subs[k][0])
                    if g == 3:
                        next_in(NB, 0)
                emit_pool_v(B, g, evw[id(B)])
                if bi == 0:
                    if g == 3:
                        S.tag = "b0.in"
                        in_transposes(16)
                        in_evac(16, hT_s, 0)
                        evw[id(BS)] = [ev_t[16]]
                    else:
                        emit_pool_v(BS, g + 1, evw[id(BS)])
                emit_pool_sums(B, g)
                if bi == 0:
                    state_in_piece(3 - g)
                    S.tag = "wdma"
                    if g == 3:
                        issue_wpiece(6)
                    elif g == 2:
                        issue_wpiece(7)
                    elif g == 1:
                        issue_wpiece(8)
                    else:
                        issue_wout(0)
                        issue_wout(1)
            if bi == 0:
                emit_pool_v(BS, 0, evw[id(BS)])
                for g in (3, 2, 1, 0):
                    emit_pool_sums(BS, g, pooled_now=False)
            for g in (3, 2, 1, 0):
                for X in group:
                    emit_zb(X, g)
                if bi == 0 and g == 3:
                    state_in_piece(4)
                if bi > 0 and NB is not None:
                    k = {3: 2, 1: 3}.get(g)
                    if k is not None and k < len(NB.subs):
                        S.tag = f"b{bi}.nxt_in"
                        in_sumsq(NB.subs[k][0])
            if bi > 0:
                S.tag = f"b{bi}.nxt_in"
                next_in(NB, 1)

            for j in range(4):
                Cs = {id(X): ConvState() for X in group}
                for X in group:
                    conv_u(X, j, Cs[id(X)])
                emit_poolw(B, 3 - j)
                for X in group:
                    conv_c(X, j, Cs[id(X)])
                for X in group:
                    conv_za(X, j, Cs[id(X)])
                for X in group:
                    conv_b(X, j, Cs[id(X)])
                if bi == 0 and j == 2:
                    emit_sample_pooled(BS)
                S.tag = f"b{bi}.nxt_tr{j}"
                if NB is not None:
                    if bi == 0:
                        if j == 0:
                            next_in(NB, 0, True); next_in(NB, 1, True)
                        elif j == 1:
                            gs, sub = NB.subs[0]
                            in_transposes(gs); in_evac(gs, NB.hT, sub)
                            next_in(NB, 2, True)
                        elif j == 2:
                            gs, sub = NB.subs[1]
                            in_transposes(gs); in_evac(gs, NB.hT, sub)
                            next_in(NB, 3, True)
                        else:
                            gs, sub = NB.subs[2]
                            in_transposes(gs); in_evac(gs, NB.hT, sub)
                    else:
                        gs, sub = NB.subs[j]
                        in_transposes(gs); in_evac(gs, NB.hT, sub)
                        next_in(NB, j + 2)
            pend = []
            for idx, (gs, sub) in enumerate(B.subs):
                pend.append(emit_out_sub(B, gs, sub, first_sub=(idx == 0 and bi == 0)))
                if len(pend) > 1:
                    pend.pop(0)()
                if bi == 0 and idx == 0:
                    S.tag = "b0.nxt_tr3"
                    gs_l, sub_l = NB.subs[3]
                    in_transposes(gs_l); in_evac(gs_l, NB.hT, sub_l)
                    emit_sample_pool_tail(BS)
                if bi == 3 and idx in (0, 1):
                    emit_state_out_sample(BS, idx)
                if bi == 0 and idx == 1:
                    pend.append(emit_out_sub(BS, 16, 0))
                    if len(pend) > 1:
                        pend.pop(0)()
            if bi == 3:
                emit_state_out_prompt(B)
            while pend:
                pend.pop(0)()

        fin = [("stout", S.count["stout"])] + [(f"y{i}", S.count[f"y{i}"]) for i in range(NH)]
        S.op("sp", lambda e: e.nop(), waits=fin, sem=None)

        sems = {name: es.enter_context(nc.semaphore(name)) for name in S.count}
        block = es.enter_context(nc.Block())

        def emit(eng_obj, name):
            known = {}
            for fn, waits, t, tag in S.ops[name]:
                for (s, v) in waits:
                    if known.get(s, 0) < v:
                        eng_obj.wait_ge(sems[s], v)
                        known[s] = v
                ins = fn(eng_obj)
                if ANNOTATE:
                    ins.annotate(tag)
                if t is not None:
                    ins.then_inc(sems[t[0]], S.inc[t[0]])

        @block.tensor
        def _(e):
            emit(e, "pe")

        @block.scalar
        def _(e):
            emit(e, "act")

        @block.vector
        def _(e):
            emit(e, "dve")

        @block.gpsimd
        def _(e):
            emit(e, "pool")

        @block.sync
        def _(e):
            emit(e, "sp")

    return nc


_NC_CACHE = {}


def _piece_major(w, pieces):
    outs = []
    for subs in pieces:
        for (c0, n) in subs:
            outs.append(w[:, c0:c0 + n].reshape(NKC, 128, n).transpose(1, 0, 2).reshape(128, NKC * n))
    return np.ascontiguousarray(np.concatenate(outs, axis=1))


def kernel(x_prompt, x_sample, state_conv, state_pool, norm_g, w_in, conv_w, pool_w, pool_scale, w_out, final_g):
    f = np.float32
    x_prompt = np.asarray(x_prompt, f); x_sample = np.asarray(x_sample, f)
    state_conv = np.asarray(state_conv, f); state_pool = np.asarray(state_pool, f)
    w_in2 = _piece_major(np.asarray(w_in, f)[0], W_PIECES)
    w_out2 = _piece_major(np.asarray(w_out, f)[0], [[(0, 512)], [(512, 512)]])
    gcols = np.ascontiguousarray(np.concatenate(
        [np.asarray(norm_g, f)[0].reshape(NKC, 128).T, np.asarray(final_g, f).reshape(NKC, 128).T], axis=1))
    cw = np.ascontiguousarray(np.asarray(conv_w, f)[0].T.reshape(4, 128, 3).transpose(1, 0, 2).reshape(128, 12))
    psc = np.ascontiguousarray(np.asarray(pool_scale, f)[0].reshape(4, 128).T)
    pw = np.ascontiguousarray(np.asarray(pool_w, f)[0])

    if "nc" not in _NC_CACHE:
        _NC_CACHE["nc"] = build_program()
    nc = _NC_CACHE["nc"]

    in_maps = []
    for c in range(N_CORES):
        sl = slice(NS * c, NS * (c + 1))
        in_maps.append({
            "xp": np.ascontiguousarray(x_prompt[c]),
            "xs": np.ascontiguousarray(x_sample[sl].transpose(1, 0, 2).reshape(NS * TPOS, D)),
            "sc": np.ascontiguousarray(state_conv[0, sl].transpose(1, 0, 2).reshape(2 * NS, 512)),
            "sp": np.ascontiguousarray(state_pool[0, sl].transpose(1, 0, 2).reshape(15 * NS, 512)),
            "gcols": gcols, "w_in": w_in2, "cw": cw, "pw": pw, "psc": psc, "w_out": w_out2,
        })
    res = run_bass_kernel_spmd(nc, in_maps, core_ids=list(range(N_CORES)))
    R = res.results
    y_prompt = np.stack([R[c]["yp"] for c in range(N_CORES)], axis=0).astype(f)
    y_sample = np.concatenate(
        [R[c]["ys"].reshape(TPOS, NS, D).transpose(1, 0, 2) for c in range(N_CORES)], axis=0).astype(f)
    ncp = np.stack([R[c]["ncp"] for c in range(N_CORES)], axis=0)[None].astype(f)
    npp = np.stack([R[c]["npp"] for c in range(N_CORES)], axis=0)[None].astype(f)
    ncs = np.concatenate([R[c]["ncs"].reshape(2, NS, 512).transpose(1, 0, 2) for c in range(N_CORES)], axis=0)[None].astype(f)
    nps = np.concatenate([R[c]["nps"].reshape(15, NS, 512).transpose(1, 0, 2) for c in range(N_CORES)], axis=0)[None].astype(f)
    return (y_prompt, y_sample, ncp, npp, ncs, nps)
```

```python
import contextlib
import os
import numpy as np
import concourse.bass as bass
import concourse.mybir as mybir
from concourse.bass_utils import run_bass_kernel_spmd

F32 = mybir.dt.float32
BF16 = mybir.dt.bfloat16
AF = mybir.ActivationFunctionType
ALU = mybir.AluOpType

N_CORES = 8
D = 1024
DIN = 3072
NKC = 8
SEQ = 2048
NS = 16
TPOS = 4
EPS = 1e-6
COL_B, COL_C, COL_U, COL_ZA, COL_V, COL_ZB = 0, 512, 1024, 1536, 2048, 2560
WINDOWS = (2, 4, 8, 16)
ANNOTATE = False
STRICT = bool(os.environ.get("KSTRICT"))

W_PIECES = [[(COL_V + 128 * g, 128)] for g in (3, 2, 1, 0)] + [[(COL_ZB, 512)]] + [
    [(COL_U + 128 * j, 128), (COL_C + 128 * j, 128), (COL_ZA + 128 * j, 128), (COL_B + 128 * j, 128)]
    for j in range(4)]


def _w_in_layout():
    off = 0
    table = {}
    piece_rng = []
    for k, subs in enumerate(W_PIECES):
        p0 = off
        for (c0, n) in subs:
            for c in range(c0, c0 + n, 128):
                table[c] = (k, off + (c - c0), n)
            off += NKC * n
        piece_rng.append((p0, off))
    return table, piece_rng, off


W_TABLE, W_RNG, W_TOT = _w_in_layout()


class Sched:
    ENGS = ("pe", "act", "dve", "pool", "sp")

    def __init__(self):
        self.ops = {e: [] for e in self.ENGS}
        self.count = {}
        self.inc = {}
        self.tag = "init"
        self.last = {}
        for e in ("pe", "act", "dve", "pool"):
            self.new_sem(e, 1)

    def new_sem(self, name, inc):
        self.count[name] = 0
        self.inc[name] = inc

    def op(self, eng, fn, waits=(), sem="default"):
        if sem == "default":
            sem = eng if eng != "sp" else None
        t = None
        if sem is not None:
            self.count[sem] += self.inc[sem]
            t = (sem, self.count[sem])
        ws = []
        if STRICT and eng in ("act", "dve", "pool") and self.last.get(eng) is not None:
            ws.append(self.last[eng])

        def flat(w):
            if w is None:
                return
            if isinstance(w, list):
                for x in w:
                    flat(x)
            else:
                ws.append(w)
        for w in waits:
            flat(w)
        self.ops[eng].append((fn, ws, t, self.tag))
        if t is not None and t[0] == eng:
            self.last[eng] = t
        return t


def build_program():
    nc = bass.Bass("TRN2", target_bir_lowering=False)
    S = Sched()

    def din(name, shape):
        return nc.dram_tensor(name, list(shape), F32, kind="ExternalInput")

    def dout(name, shape):
        return nc.dram_tensor(name, list(shape), F32, kind="ExternalOutput")

    xp_h = din("xp", [SEQ, D]); xs_h = din("xs", [NS * TPOS, D])
    sc_h = din("sc", [2 * NS, 512]); sp_h = din("sp", [15 * NS, 512])
    gc_h = din("gcols", [128, 16]); win_h = din("w_in", [128, W_TOT]); cw_h = din("cw", [128, 12])
    pw_h = din("pw", [4, 128, 128]); psc_h = din("psc", [128, 4]); wout_h = din("w_out", [128, NKC * D])
    yp_h = dout("yp", [SEQ, D]); ys_h = dout("ys", [NS * TPOS, D])
    ncp_h = dout("ncp", [2, 512]); npp_h = dout("npp", [15, 512])
    ncs_h = dout("ncs", [2 * NS, 512]); nps_h = dout("nps", [15 * NS, 512])
    xp, xs_d, sc_d, sp_d = xp_h.ap(), xs_h.ap(), sc_h.ap(), sp_h.ap()
    yp, ys_d = yp_h.ap(), ys_h.ap()
    win_d = win_h.ap()
    wout_d = wout_h.ap()
    pw_v = pw_h.ap().rearrange("g c d -> c g d")

    es = contextlib.ExitStack()
    with es:
        def sb(name, shape, dt=F32):
            return es.enter_context(nc.sbuf_tensor(name, list(shape), dt))

        def ps(name, shape, dt=F32):
            return es.enter_context(nc.psum_tensor(name, list(shape), dt))

        w_in = sb("w_in_bf", [128, W_TOT], BF16)
        w_out = sb("w_out_bf", [128, 2, NKC, 512], BF16)
        pw = sb("pw_bf", [128, 4, 128], BF16)
        gb = sb("gb", [128, D]); fgb = sb("fgb", [128, D])
        gcols = sb("gcols_sb", [128, 16])
        cw = sb("cw_sb", [128, 12]); psc = sb("psc_sb", [128, 4])
        ident_bf = sb("ident_bf", [128, 128], BF16); ident_f = sb("ident_f", [128, 128])
        invcnt = sb("invcnt", [128, 16]); neghalf = sb("neghalf", [128, 1])
        tmpfix = sb("tmpfix", [128, 16])
        NXS = 8
        xt = [sb(f"xt{i}", [128, D]) for i in range(NXS)]
        xt_s = sb("xt_s", [128, D])
        xsb = [sb(f"xsb{i}", [128, D], BF16) for i in range(2)]
        hT = [sb(f"hT{i}", [128, NKC, 512], BF16) for i in range(2)]
        hT_s = sb("hT_s", [128, NKC, 64], BF16)
        NSUB = 17
        ssx = sb("ssx", [128, NSUB]); rx = sb("rx", [128, NSUB])
        ssh = sb("ssh", [128, NSUB]); rh = sb("rh", [128, NSUB])
        sa = [sb(f"sa{i}", [128, 512]) for i in range(2)]
        Ab = [sb(f"A{i}", [128, 512]) for i in range(2)]
        sa_s = sb("sa_s", [128, 64]); Ab_s = sb("Ab_s", [128, 64])
        cu_ext = [sb(f"cu_ext{j}", [128, 514]) for j in range(4)]
        cu_ext_s = sb("cu_ext_s", [128, 4, 96])
        v_ext = [sb(f"v_ext{g}", [128, 527]) for g in range(4)]
        v_ext_s = sb("v_ext_s", [128, 4, 304])
        sbz = [sb(f"sbz{g}", [128, 512]) for g in range(4)]
        sbz_s = sb("sbz_s", [128, 4, 64])
        wsb = {2: sb("s2buf", [128, 527]), 4: sb("s4buf", [128, 527]),
               8: sb("s8buf", [128, 527]), 16: sb("s16buf", [128, 527])}
        pooled = [sb(f"pooled{g}", [128, 512], BF16) for g in range(4)]
        pooledS = [sb(f"pooledS{g}", [128, 64], BF16) for g in range(4)]
        mix = sb("mix", [128, NKC, 512], BF16)
        mix_s = sb("mix_s", [128, NKC, 64], BF16)
        NH = 3
        hsb = [sb(f"hsb{i}", [128, D]) for i in range(NH)]
        junk = sb("junk", [128, D], BF16)

        pT = ps("pT", [128, 1024], BF16)
        pP = [ps(f"pP{i}", [128, 512]) for i in range(4)]
        pW = ps("pW", [128, 512])
        pO = [ps("pO0", [128, 512]), ps("pO1", [128, 512])]
        pT3 = pT[:].rearrange("p (k t) -> p k t", k=NKC)
        pO0b3 = pO[0].bitcast(BF16)[:].rearrange("p (k t) -> p k t", k=NKC)

        for i in range(NXS):
            S.new_sem(f"x{i}", 16)
        S.new_sem("xs16", 16)
        for i in range(NH):
            S.new_sem(f"y{i}", 16)
        S.new_sem("const", 16); S.new_sem("stin", 16); S.new_sem("stout", 16); S.new_sem("gbs", 16)
        S.new_sem("pwd", 16)

        def ACT(out, in_, func, waits, scale=None, accum=None):
            kw = {}
            if scale is not None:
                kw["scale"] = scale
            if accum is not None:
                kw["accum_out"] = accum
            return S.op("act", lambda e: e.activation(out=out, in_=in_, func=func, **kw), waits=waits)

        def TT(eng, out, in0, in1, op, waits):
            return S.op(eng, lambda e: e.tensor_tensor(out=out, in0=in0, in1=in1, op=op), waits=waits)

        def STT(out, in0, scalar, in1, op0, op1, waits):
            return S.op("dve", lambda e: e.scalar_tensor_tensor(out=out, in0=in0, scalar=scalar, in1=in1,
                                                                op0=op0, op1=op1), waits=waits)

        def TS(eng, out, in0, s1, s2, op0, op1, waits):
            return S.op(eng, lambda e: e.tensor_scalar(out=out, in0=in0, scalar1=s1, scalar2=s2, op0=op0, op1=op1),
                        waits=waits)

        def CP(eng, out, in_, waits):
            return S.op(eng, lambda e: e.tensor_copy(out=out, in_=in_), waits=waits)

        def MSET(eng, ap, val, waits=()):
            return S.op(eng, lambda e: e.memset(ap, val), waits=waits)

        def MM(out, lhsT, rhs, start, stop, waits=(), inc=False):
            return S.op("pe", lambda e: e.matmul(out, lhsT=lhsT, rhs=rhs, start=start, stop=stop),
                        waits=waits, sem=("pe" if inc else None))

        def TR(out, in_, ident, waits=(), inc=False):
            return S.op("pe", lambda e: e.transpose(out=out, in_=in_, identity=ident),
                        waits=waits, sem=("pe" if inc else None))

        def DMA(eng, out, in_, waits, sem):
            return S.op(eng, lambda e: e.dma_start(out=out, in_=in_), waits=waits, sem=sem)

        t_m1 = MSET("pool", ident_bf[:], 0.0)
        t_m2 = MSET("pool", ident_f[:], 0.0)
        t_nh0 = MSET("pool", neghalf[:], -0.5)
        t_identb = S.op("pool", lambda e: e.affine_select(
            out=ident_bf[:], in_=ident_bf[:], compare_op=ALU.not_equal, fill=1.0, base=0,
            pattern=[[-1, 128]], channel_multiplier=1), waits=[t_m1])
        t_identf = S.op("pool", lambda e: e.affine_select(
            out=ident_f[:], in_=ident_f[:], compare_op=ALU.not_equal, fill=1.0, base=0,
            pattern=[[-1, 128]], channel_multiplier=1), waits=[t_m2, t_nh0])

        junk_t = [ACT(junk[:, 0:1], neghalf[:, 0:1], AF.Silu, [t_nh0])]

        t_halo = None
        for g in range(4):
            MSET("dve", v_ext[g][:, 0:15], 0.0)
            MSET("dve", cu_ext[g][:, 0:2], 0.0)
        for t in range(16):
            t_halo = MSET("dve", invcnt[:, t:t + 1], 1.0 / (t + 1))
        t_dve_init = t_halo

        wt = {}
        wo_t = {}

        def issue_wpiece(k, waits=()):
            a, b_ = W_RNG[k]
            name = f"w{k}"
            S.new_sem(name, 16)
            wt[k] = DMA("pool", w_in[:, a:b_], win_d[:, a:b_], list(waits), name)

        def issue_wout(h):
            name = f"wo{h}"
            S.new_sem(name, 16)
            wo_t[h] = DMA("pool", w_out[:, h, :, :].rearrange("p k c -> p (k c)"),
                          wout_d[:, h * NKC * 512:(h + 1) * NKC * 512], [], name)

        def w_lhsT(ecol, kc):
            k, off, n = W_TABLE[ecol]
            return w_in[:, off + kc * n: off + kc * n + 128], wt[k]

        def xrows(gs):
            if gs < 16:
                return xp[gs * 128:(gs + 1) * 128, :], 128
            return xs_d[:, :], 64

        def yrows(gs):
            if gs < 16:
                return yp[gs * 128:(gs + 1) * 128, :], 128
            return ys_d[:, :], 64

        def xtile(gs):
            return xt[gs % NXS] if gs < 16 else xt_s

        x_t = {}
        h_t = {}

        def load_x(gs, extra=(), eng="sp"):
            src, m = xrows(gs)
            if gs == 16:
                x_t[gs] = DMA("sp", xt_s[0:m, :], src, list(extra), "xs16")
                return
            slot = gs % NXS
            x_t[gs] = DMA("sp", xt[slot][0:m, :], src, [h_t.get(gs - NXS)] + list(extra), f"x{slot}")

        t_gc = DMA("sp", gcols[:], gc_h.ap(), [], "gbs")
        for gs in range(4):
            load_x(gs)
        load_x(16)
        DMA("sp", cw[:], cw_h.ap(), [], "const")
        t_const = DMA("sp", psc[:], psc_h.ap(), [], "const")
        DMA("sp", hsb[0][:, 0:512], sp_d[0:128, :], [], "stin")
        DMA("sp", hsb[0][0:112, 512:1024], sp_d[128:240, :], [], "stin")
        t_stin = DMA("sp", hsb[1][0:32, 0:512], sc_d[:, :], [], "stin")

        t_pw_box = [None]

        bankP_free = [None] * 4
        bankW_free = [None]
        bankO_free = [None, None]
        bankT_free = [None]
        xsb_free = [None, None]
        hsb_free = [None] * NH
        sbz_free = {}
        saA_free = {}
        cu_roll = [t_dve_init] * 4
        v_roll = [t_dve_init] * 4
        wsb_free = {2: None, 4: None, 8: None, 16: None}
        wsb_pool_rw = {}
        fix_t = [None]
        ss_t = {}; r_t = {}; xs_t = {}; tr_t = {}; ev_t = {}
        chunk_ctr = [0]
        conv_ctr = [0]
        xs_ctr = [0]
        h_ctr = [0]
        xs_q = {}

        S.tag = "gains"
        t_gb = []
        t_fgb = []
        for which, dst, banks in ((0, gb, (pO[0], pO[1])), (1, fgb, (pP[0], pP[1]))):
            for half in range(2):
                t = None
                for k4 in range(4):
                    kc = half * 4 + k4
                    src = bass.AP(gcols, which * 8 + kc, [[16, 128], [0, 128]])
                    t = TR(banks[half][:, k4 * 128:(k4 + 1) * 128], src, ident_f[:, :],
                           waits=([t_gc, t_identf] if k4 == 0 else ()), inc=(k4 == 3))
                cs = slice(half * 512, (half + 1) * 512)
                if which == 0:
                    tc = CP("dve", dst[:, cs], banks[half][:, :], [t])
                    bankO_free[half] = tc
                    t_gb.append(tc)
                else:
                    tc = CP("dve", dst[:, cs], banks[half][:, :], [t])
                    bankP_free[half] = tc
                    t_fgb.append(tc)

        def in_sumsq(gs):
            if gs in ss_t:
                return
            _, m = xrows(gs)
            ss_t[gs] = ACT(junk[0:m, :], xtile(gs)[0:m, :], AF.Square, [x_t[gs], junk_t[0]], accum=ssx[0:m, gs:gs + 1])
            junk_t[0] = ss_t[gs]

        def in_r(gs):
            _, m = xrows(gs)
            t1 = TS("pool", rx[0:m, gs:gs + 1], ssx[0:m, gs:gs + 1], 1.0 / D, EPS, ALU.mult, ALU.add, [ss_t[gs]])
            r_t[gs] = TT("pool", rx[0:m, gs:gs + 1], rx[0:m, gs:gs + 1], neghalf[0:m, :], ALU.pow, [t1])

        def in_xs(gs, on_pool=False):
            _, m = xrows(gs)
            q = xs_ctr[0] % 2
            xs_ctr[0] += 1
            xs_q[gs] = q
            if on_pool:
                tl = None
                for hf in range(2):
                    cs = slice(hf * 512, (hf + 1) * 512)
                    t1 = TS("pool", hsb[2][0:m, cs], xtile(gs)[0:m, cs], rx[0:m, gs:gs + 1], 1.0, ALU.mult, ALU.mult,
                            [r_t[gs], hsb_free[2]])
                    tl = TT("pool", xsb[q][0:m, cs], hsb[2][0:m, cs], gb[0:m, cs], ALU.mult, [t1, t_gb, xsb_free[q]])
                xs_t[gs] = tl
                hsb_free[2] = tl
                return
            xs_t[gs] = STT(xsb[q][0:m, :], xtile(gs)[0:m, :], rx[0:m, gs:gs + 1], gb[0:m, :], ALU.mult, ALU.mult,
                           [r_t[gs], t_gb, xsb_free[q]])

        def in_transposes(gs, alt=False):
            _, m = xrows(gs)
            q = xs_q[gs]
            bank3 = pO0b3 if alt else pT3
            free = (bankO_free[0] if alt else bankT_free[0])
            t = None
            for kc in range(NKC):
                t = TR(bank3[:, kc, 0:m], xsb[q][0:m, kc * 128:(kc + 1) * 128], ident_bf[0:m, 0:m],
                       waits=([xs_t[gs], free, t_identb] if kc == 0 else ()), inc=(kc == NKC - 1))
            tr_t[gs] = t
            xsb_free[q] = t

        def in_evac(gs, dst3, sub, alt=False):
            _, m = xrows(gs)
            c0 = sub * 128
            dst = dst3[:, :, c0:c0 + m]
            if alt:
                ev_t[gs] = CP("dve", dst, pO0b3[:, :, 0:m], [tr_t[gs]])
                bankO_free[0] = ev_t[gs]
            else:
                ev_t[gs] = ACT(dst, pT3[:, :, 0:m], AF.Copy, [tr_t[gs]])
                bankT_free[0] = ev_t[gs]

        def next_in(NB, idx, on_pool=False):
            if NB is None or idx >= len(NB.subs):
                return
            gs, _ = NB.subs[idx]
            in_sumsq(gs)
            in_r(gs)
            in_xs(gs, on_pool=on_pool)

        class Blk:
            pass

        def mk(kind, b, bi, ntok, U, T, subs, hTb, mixb):
            B = Blk()
            B.kind = kind; B.b = b; B.bi = bi; B.ntok = ntok; B.U = U; B.T = T; B.subs = subs
            B.is_p = kind == "p"; B.hT = hTb; B.mix = mixb
            B.hw_c = 2 * U; B.hw_v = 15 * U
            B.pooled_t = {}; B.silu_b_t = {}; B.ya_t = {}; B.yb_t = {}
            B.pooled = pooled if B.is_p else pooledS
            return B

        Bs = [mk("p", b, b, 512, 1, 512, [(4 * b + s, s) for s in range(4)], hT[b % 2], mix) for b in range(4)]
        BS = mk("s", 4, 4, 64, 16, 4, [(16, 0)], hT_s, mix_s)

        def cuE(B, j, a, b_):
            return cu_ext[j][:, a:b_] if B.is_p else cu_ext_s[:, j, a:b_]

        def vE(B, g, a, b_):
            return v_ext[g][:, a:b_] if B.is_p else v_ext_s[:, g, a:b_]

        def sbzE(B, g):
            return sbz[g][:, 0:B.ntok] if B.is_p else sbz_s[:, g, :]

        t_state = []

        def state_in_piece(i):
            S.tag = "state_in"
            bank, getf, setf = [
                (pO[1], lambda: bankO_free[1], lambda t: bankO_free.__setitem__(1, t)),
                (pP[2], lambda: bankP_free[2], lambda t: bankP_free.__setitem__(2, t)),
                (pP[3], lambda: bankP_free[3], lambda t: bankP_free.__setitem__(3, t)),
                (pW, lambda: bankW_free[0], lambda t: bankW_free.__setitem__(0, t)),
                (pP[1], lambda: bankP_free[1], lambda t: bankP_free.__setitem__(1, t)),
            ][i]
            if i < 4:
                g = i
                TR(bank[:, 0:128], hsb[0][:, g * 128:(g + 1) * 128], ident_f[:, :],
                   waits=[t_stin, t_identf, getf()])
                t = TR(bank[:, 128:240], hsb[0][0:112, 512 + g * 128:512 + (g + 1) * 128], ident_f[0:112, 0:112], inc=True)
                tc = ACT(v_ext_s[:, g, 0:240], bank[:, 0:240], AF.Copy, [t])
                if i == 3:
                    hsb_free[0] = tc
            else:
                t = None
                for j in range(4):
                    t = TR(bank[:, j * 32:(j + 1) * 32], hsb[1][0:32, j * 128:(j + 1) * 128], ident_f[0:32, 0:32],
                           waits=([t_stin, t_identf, getf()] if j == 0 else ()), inc=(j == 3))
                tc = ACT(cu_ext_s[:, :, 0:32], bank[:, 0:128].rearrange("p (j c) -> p j c", j=4), AF.Copy, [t])
                hsb_free[1] = tc
            setf(tc)
            t_state.append(tc)

        S.tag = "b0.in"
        first = [gs for gs, _ in Bs[0].subs] + [16]
        for gs in first:
            in_sumsq(gs)
        for i, gs in enumerate(first):
            alt = (i % 2 == 1)
            if i == 2:
                issue_wpiece(0, waits=[x_t[2]])
            in_r(gs)
            if i == 3:
                for k in range(1, 4):
                    issue_wpiece(k)
                t_pw_box[0] = DMA("pool", pw[:], pw_v, [], "pwd")
            in_xs(gs)
            if gs < 16:
                in_transposes(gs, alt=alt)
                in_evac(gs, Bs[0].hT, gs % 4, alt=alt)
        for i in range(5):
            state_in_piece(i)
        S.tag = "wdma"
        issue_wpiece(4)
        issue_wpiece(5)
        for gs in range(4, 8):
            load_x(gs, extra=[wt[4] if gs < 6 else wt[5]])

        store_t = {}

        def proj_chunk(B, ecol, extra_waits=()):
            i = chunk_ctr[0] % 4
            chunk_ctr[0] += 1
            ntok = B.ntok
            t = None
            for kc in range(NKC):
                lhsT, wtick = w_lhsT(ecol, kc)
                t = MM(pP[i][:, 0:ntok], lhsT, B.hT[:, kc, 0:ntok],
                       start=(kc == 0), stop=(kc == NKC - 1),
                       waits=([bankP_free[i], wtick] + list(extra_waits) if kc == 0 else ()),
                       inc=(kc == NKC - 1))
            return i, t

        def emit_pool_v(B, g, evw):
            ntok = B.ntok
            S.tag = f"b{B.bi}.v{g}"
            i, tp = proj_chunk(B, COL_V + 128 * g, extra_waits=evw)
            tv = ACT(vE(B, g, B.hw_v, B.hw_v + ntok), pP[i][:, 0:ntok], AF.Copy,
                     [tp, v_roll[g] if B.is_p else None])
            bankP_free[i] = tv
            B.tv = getattr(B, "tv", {})
            B.tv[g] = tv

        def emit_pool_sums(B, g, pooled_now=True):
            ntok, U, T = B.ntok, B.U, B.T
            S.tag = f"b{B.bi}.v{g}"
            w = WINDOWS[g]
            need = {w: 15}
            ww = w
            while ww > 2:
                need[ww // 2] = need[ww] - ww // 2
                ww //= 2
            prev_t = [B.tv[g]] + ([] if B.is_p else list(t_state))
            for ww in sorted(need):
                lo = need[ww]; hi = 15 + T
                sh = ww // 2
                if ww == 2:
                    in0 = vE(B, g, lo * U, hi * U); in1 = vE(B, g, (lo - 1) * U, (hi - 1) * U)
                else:
                    src = wsb[ww // 2]
                    in0 = src[:, lo * U:hi * U]; in1 = src[:, (lo - sh) * U:(hi - sh) * U]
                prev_t = TT("pool", wsb[ww][:, lo * U:hi * U], in0, in1, ALU.add,
                            [prev_t, wsb_free[ww], wsb_pool_rw.get(ww)])
                wsb_pool_rw[ww] = prev_t
                if ww > 2:
                    wsb_pool_rw[ww // 2] = prev_t
            fin = wsb[w]
            if not pooled_now:
                B.wsum_t = getattr(B, "wsum_t", {})
                B.wsum_t[g] = prev_t
                return
            tpo = STT(B.pooled[g][:, 0:ntok], fin[:, 15 * U:(15 + T) * U], 1.0 / w, vE(B, g, B.hw_v, B.hw_v + ntok),
                      ALU.mult, ALU.subtract, [prev_t])
            if B.is_p and B.b == 0:
                nfx = w - 1
                tf1 = TT("dve", tmpfix[:, 0:nfx], fin[:, 15:15 + nfx], invcnt[:, 0:nfx], ALU.mult, [tpo, fix_t[0]])
                tpo = TT("dve", pooled[g][:, 0:nfx], tmpfix[:, 0:nfx], v_ext[g][:, 15:15 + nfx], ALU.subtract, [tf1])
                fix_t[0] = tpo
            B.pooled_t[g] = tpo
            wsb_free[w] = tpo
            if B.is_p and B.b < 3:
                v_roll[g] = CP("dve", v_ext[g][:, 0:15], v_ext[g][:, 512:527], [tpo])

        def emit_zb(B, g):
            S.tag = f"b{B.bi}.zB{g}"
            i, tp = proj_chunk(B, COL_ZB + 128 * g)
            ts_ = ACT(sbzE(B, g), pP[i][:, 0:B.ntok], AF.Silu, [tp, sbz_free.get((B.is_p, g))])
            bankP_free[i] = ts_
            B.silu_b_t[g] = ts_

        def emit_poolw(B, g):
            ntok = B.ntok
            S.tag = f"b{B.bi}.poolw{g}"
            tpw = MM(pW[:, 0:ntok], pw[:, g, :], B.pooled[g][:, 0:ntok], True, True,
                     waits=[B.pooled_t[g], bankW_free[0], t_pw_box[0]], inc=True)
            tyb = STT(B.mix[:, 4 + g, 0:ntok], pW[:, 0:ntok], psc[:, g:g + 1], sbzE(B, g),
                      ALU.mult, ALU.mult, [tpw, B.silu_b_t[g], t_const])
            bankW_free[0] = tyb
            sbz_free[(B.is_p, g)] = tyb
            B.yb_t[g] = tyb

        def emit_sample_pooled(B):
            ntok, U, T = B.ntok, B.U, B.T
            S.tag = "b4.pooled"
            for g in (3, 2, 1, 0):
                w = WINDOWS[g]
                tpo = STT(B.pooled[g][:, 0:ntok], wsb[w][:, 15 * U:(15 + T) * U], 1.0 / w,
                          vE(B, g, B.hw_v, B.hw_v + ntok), ALU.mult, ALU.subtract, [B.wsum_t[g]])
                B.pooled_t[g] = tpo
                wsb_free[w] = tpo

        def emit_sample_pool_tail(B):
            ntok = B.ntok
            S.tag = "b4.pooltail"
            tp = None
            for g in (3, 2, 1, 0):
                tp = MM(pW[:, g * 64:(g + 1) * 64], pw[:, g, :], B.pooled[g][:, 0:ntok], True, True,
                        waits=([bankW_free[0], t_pw_box[0]] + [B.pooled_t[q] for q in range(4)] if g == 3 else ()),
                        inc=(g == 0))
            tyb = None
            for g in (3, 2, 1, 0):
                tyb = STT(B.mix[:, 4 + g, 0:ntok], pW[:, g * 64:(g + 1) * 64], psc[:, g:g + 1], sbzE(B, g),
                          ALU.mult, ALU.mult, [tp, B.silu_b_t[g], t_const])
                B.yb_t[g] = tyb
                sbz_free[(B.is_p, g)] = tyb
            bankW_free[0] = tyb

        class ConvState:
            pass

        def conv_bufs(B):
            if B.is_p:
                k = conv_ctr[0] % 2
                conv_ctr[0] += 1
                return sa[k][:, 0:B.ntok], Ab[k][:, 0:B.ntok], ("p", k)
            return sa_s[:, :], Ab_s[:, :], ("s", 0)

        def conv_u(B, j, C):
            S.tag = f"b{B.bi}.conv{j}"
            C.sa, C.A, C.key = conv_bufs(B)
            C.body = cuE(B, j, B.hw_c, B.hw_c + B.ntok)
            iu, tpu = proj_chunk(B, COL_U + 128 * j)
            C.tu = ACT(C.body, pP[iu][:, 0:B.ntok], AF.Copy, [tpu, cu_roll[j] if B.is_p else None])
            bankP_free[iu] = C.tu

        def conv_taps12(B, j, C, tt1):
            ntok, U = B.ntok, B.U
            tt2 = STT(C.A, cuE(B, j, U, U + ntok), cw[:, 3 * j + 1:3 * j + 2], C.A, ALU.mult, ALU.add,
                      [tt1, C.tcu] + ([] if B.is_p else list(t_state)))
            C.tyc = STT(C.A, cuE(B, j, 2 * U, 2 * U + ntok), cw[:, 3 * j + 2:3 * j + 3], C.A, ALU.mult, ALU.add, [tt2])
            if B.is_p and B.b < 3:
                cu_roll[j] = CP("dve", cu_ext[j][:, 0:2], cu_ext[j][:, 512:514], [C.tyc])

        def conv_c(B, j, C):
            S.tag = f"b{B.bi}.conv{j}"
            ntok = B.ntok
            ic, tpc = proj_chunk(B, COL_C + 128 * j)
            C.tcu = TT("dve", C.body, pP[ic][:, 0:ntok], C.body, ALU.mult, [tpc, C.tu])
            bankP_free[ic] = C.tcu
            C.tt1 = None
            if j == 3 and B.is_p and B.bi > 0:
                C.tt1 = TS("pool", C.A, cuE(B, j, 0, ntok), cw[:, 3 * j:3 * j + 1], 1.0, ALU.mult, ALU.mult,
                           [C.tcu, saA_free.get(C.key), t_const])

        def conv_za(B, j, C):
            S.tag = f"b{B.bi}.conv{j}"
            iz, tpz = proj_chunk(B, COL_ZA + 128 * j)
            C.tsl = ACT(C.sa, pP[iz][:, 0:B.ntok], AF.Silu, [tpz, saA_free.get(C.key)])
            bankP_free[iz] = C.tsl
            if C.tt1 is not None:
                conv_taps12(B, j, C, C.tt1)

        def conv_b(B, j, C):
            S.tag = f"b{B.bi}.conv{j}"
            ntok = B.ntok
            ib, tpb = proj_chunk(B, COL_B + 128 * j)
            tgt = TT("dve", C.sa, pP[ib][:, 0:ntok], C.sa, ALU.mult, [tpb, C.tsl])
            bankP_free[ib] = tgt
            if C.tt1 is None:
                tt1 = ACT(C.A, cuE(B, j, 0, ntok), AF.Copy,
                          [C.tcu, saA_free.get(C.key), t_const] + ([] if B.is_p else list(t_state)),
                          scale=cw[:, 3 * j:3 * j + 1])
                conv_taps12(B, j, C, tt1)
            tya = TT("dve", B.mix[:, j, 0:ntok], C.A, C.sa, ALU.mult, [C.tyc, tgt])
            B.ya_t[j] = tya
            saA_free[C.key] = tya

        def emit_out_sub(B, gs, sub, first_sub=False):
            _, m = xrows(gs)
            c0 = sub * 128
            hk = h_ctr[0] % NH
            h_ctr[0] += 1
            xtl = xtile(gs)
            S.tag = f"b{B.bi}.out{sub}"
            korder = [7, 6, 5, 4, 0, 1, 2, 3]
            kwait = {4: B.yb_t[0], 5: B.yb_t[1], 6: B.yb_t[2], 7: B.yb_t[3],
                     0: B.ya_t[0], 1: B.ya_t[1], 2: B.ya_t[2], 3: B.ya_t[3]}
            tl = [None, None]
            if first_sub:
                seq = [(half, idx, kc) for half in range(2) for idx, kc in enumerate(korder[:-1])]
                seq += [(half, NKC - 1, korder[-1]) for half in range(2)]
            else:
                seq = [(half, idx, kc) for half in range(2) for idx, kc in enumerate(korder)]
            for (half, idx, kc) in seq:
                w = [kwait[kc]]
                if idx == 0:
                    w += [bankO_free[half], wo_t[half]]
                tl[half] = MM(pO[half][0:m, :], B.mix[:, kc, c0:c0 + m], w_out[:, half, kc, :],
                              (idx == 0), (idx == NKC - 1), waits=w, inc=(idx == NKC - 1))
            ths = []
            for half in range(2):
                th = TT("dve", hsb[hk][0:m, half * 512:(half + 1) * 512], pO[half][0:m, :],
                        xtl[0:m, half * 512:(half + 1) * 512], ALU.add, [tl[half], hsb_free[hk]])
                bankO_free[half] = th
                ths.append(th)
            h_t[gs] = ths[1]
            tsq = ACT(junk[0:m, :], hsb[hk][0:m, :], AF.Square, ths + [junk_t[0]], accum=ssh[0:m, gs:gs + 1])
            junk_t[0] = tsq
            t1 = TS("pool", rh[0:m, gs:gs + 1], ssh[0:m, gs:gs + 1], 1.0 / D, EPS, ALU.mult, ALU.add, [tsq])
            tr2 = TT("pool", rh[0:m, gs:gs + 1], rh[0:m, gs:gs + 1], neghalf[0:m, :], ALU.pow, [t1])

            def emit_yout():
                S.tag = f"b{B.bi}.yout{sub}"
                ty = STT(hsb[hk][0:m, :], hsb[hk][0:m, :], rh[0:m, gs:gs + 1], fgb[0:m, :], ALU.mult, ALU.mult,
                         [tr2, tsq, t_fgb])
                dst, _ = yrows(gs)
                tst = DMA("sp", dst, hsb[hk][0:m, :], [ty], f"y{hk}")
                hsb_free[hk] = tst
                store_t[gs] = tst
                nxt_gs = gs + NXS
                if nxt_gs < 16:
                    load_x(nxt_gs)
            return emit_yout

        def emit_state_out_prompt(B):
            S.tag = "p.state_out"
            t = None
            for g in range(4):
                t = TR(pP[0][0:15, g * 128:(g + 1) * 128], v_ext[g][:, 512:527], ident_f[:, :],
                       waits=([bankP_free[0]] + [B.pooled_t[q] for q in range(4)] if g == 0 else ()), inc=(g == 3))
            tc = ACT(xt[1][0:15, 0:512], pP[0][0:15, :], AF.Copy, [t, h_t[9]])
            bankP_free[0] = tc
            DMA("sp", npp_h.ap(), xt[1][0:15, 0:512], [tc], "stout")
            for j in range(4):
                t = TR(pP[1][0:2, j * 128:(j + 1) * 128], cu_ext[j][:, 512:514], ident_f[:, :],
                       waits=([bankP_free[1]] + [B.ya_t[q] for q in range(4)] if j == 0 else ()), inc=(j == 3))
            tc = ACT(xt[1][0:2, 512:1024], pP[1][0:2, :], AF.Copy, [t, h_t[9]])
            bankP_free[1] = tc
            DMA("sp", ncp_h.ap(), xt[1][0:2, 512:1024], [tc], "stout")

        def emit_state_out_sample(B, part):
            S.tag = "s.state_out"
            t = None
            if part == 0:
                for j in range(4):
                    t = TR(pP[2][0:32, j * 128:(j + 1) * 128], cu_ext_s[:, j, 64:96], ident_f[:, :],
                           waits=([bankP_free[2]] + [B.ya_t[q] for q in range(4)] if j == 0 else ()), inc=(j == 3))
                tc = ACT(xt[2][0:32, 0:512], pP[2][0:32, :], AF.Copy, [t, h_t[10]])
                bankP_free[2] = tc
                DMA("sp", ncs_h.ap(), xt[2][0:32, 0:512], [tc], "stout")
                for g in range(4):
                    t = TR(pP[3][:, g * 128:(g + 1) * 128], v_ext_s[:, g, 64:192], ident_f[:, :],
                           waits=([bankP_free[3]] + [B.pooled_t[q] for q in range(4)] if g == 0 else ()), inc=(g == 3))
                tc = ACT(xt[3][:, 0:512], pP[3][:, :], AF.Copy, [t, h_t[11]])
                bankP_free[3] = tc
                DMA("sp", nps_h.ap()[0:128, :], xt[3][:, 0:512], [tc], "stout")
            else:
                for g in range(4):
                    t = TR(pP[2][0:112, g * 128:(g + 1) * 128], v_ext_s[:, g, 192:304], ident_f[:, :],
                           waits=([bankP_free[2]] if g == 0 else ()), inc=(g == 3))
                tc = ACT(xt[2][0:112, 512:1024], pP[2][0:112, :], AF.Copy, [t, h_t[10]])
                bankP_free[2] = tc
                DMA("sp", nps_h.ap()[128:240, :], xt[2][0:112, 512:1024], [tc], "stout")

        for bi in range(4):
            B = Bs[bi]
            NB = Bs[bi + 1] if bi + 1 < 4 else None
            group = [B, BS] if bi == 0 else [B]
            evw = {id(X): [ev_t.get(gs) for gs, _ in X.subs] for X in group}

            for g in (3, 2, 1, 0):
                if bi > 0 and NB is not None:
                    k = {3: 0, 1: 1}.get(g)
                    if k is not None and k < len(NB.subs):
                        S.tag = f"b{bi}.nxt_in"
                        in_sumsq(NB.subs[k][0])
                    if g == 3:
                        next_in(NB, 0)
                emit_pool_v(B, g, evw[id(B)])
                if bi == 0:
                    if g == 3:
                        S.tag = "b0.in"
                        in_transposes(16)
                        in_evac(16, hT_s, 0)
                        evw[id(BS)] = [ev_t[16]]
                    else:
                        emit_pool_v(BS, g + 1, evw[id(BS)])
                emit_pool_sums(B, g)
                if bi == 0:
                    S.tag = "wdma"
                    if g == 3:
                        issue_wpiece(6)
                    elif g == 2:
                        issue_wpiece(7)
                    elif g == 1:
                        issue_wpiece(8)
                    else:
                        issue_wout(0)
                        issue_wout(1)
            if bi == 0:
                emit_pool_v(BS, 0, evw[id(BS)])
                for g in (3, 2, 1, 0):
                    emit_pool_sums(BS, g, pooled_now=False)
            for g in (3, 2, 1, 0):
                for X in group:
                    emit_zb(X, g)
                if bi > 0 and NB is not None:
                    k = {3: 2, 1: 3}.get(g)
                    if k is not None and k < len(NB.subs):
                        S.tag = f"b{bi}.nxt_in"
                        in_sumsq(NB.subs[k][0])
            if bi > 0:
                S.tag = f"b{bi}.nxt_in"
                next_in(NB, 1)

            for j in range(4):
                Cs = {id(X): ConvState() for X in group}
                for X in group:
                    conv_u(X, j, Cs[id(X)])
                emit_poolw(B, 3 - j)
                for X in group:
                    conv_c(X, j, Cs[id(X)])
                for X in group:
                    conv_za(X, j, Cs[id(X)])
                for X in group:
                    conv_b(X, j, Cs[id(X)])
                if bi == 0 and j == 2:
                    emit_sample_pooled(BS)
                S.tag = f"b{bi}.nxt_tr{j}"
                if NB is not None:
                    if bi == 0:
                        if j == 0:
                            next_in(NB, 0, True); next_in(NB, 1, True)
                        elif j == 1:
                            gs, sub = NB.subs[0]
                            in_transposes(gs); in_evac(gs, NB.hT, sub)
                            next_in(NB, 2, True)
                        elif j == 2:
                            gs, sub = NB.subs[1]
                            in_transposes(gs); in_evac(gs, NB.hT, sub)
                            next_in(NB, 3, True)
                        else:
                            gs, sub = NB.subs[2]
                            in_transposes(gs); in_evac(gs, NB.hT, sub)
                    else:
                        gs, sub = NB.subs[j]
                        in_transposes(gs); in_evac(gs, NB.hT, sub)
                        next_in(NB, j + 2)
            pend = []
            for idx, (gs, sub) in enumerate(B.subs):
                pend.append(emit_out_sub(B, gs, sub, first_sub=(idx == 0 and bi == 0)))
                if len(pend) > 1:
                    pend.pop(0)()
                if bi == 0 and idx == 0:
                    S.tag = "b0.nxt_tr3"
                    gs_l, sub_l = NB.subs[3]
                    in_transposes(gs_l); in_evac(gs_l, NB.hT, sub_l)
                    emit_sample_pool_tail(BS)
                if bi == 3 and idx in (0, 1):
                    emit_state_out_sample(BS, idx)
                if bi == 0 and idx == 1:
                    pend.append(emit_out_sub(BS, 16, 0))
                    if len(pend) > 1:
                        pend.pop(0)()
            if bi == 3:
                emit_state_out_prompt(B)
            while pend:
                pend.pop(0)()

        fin = [("stout", S.count["stout"])] + [(f"y{i}", S.count[f"y{i}"]) for i in range(NH)]
        S.op("sp", lambda e: e.nop(), waits=fin, sem=None)

        sems = {name: es.enter_context(nc.semaphore(name)) for name in S.count}
        block = es.enter_context(nc.Block())

        def emit(eng_obj, name):
            known = {}
            for fn, waits, t, tag in S.ops[name]:
                for (s, v) in waits:
                    if known.get(s, 0) < v:
                        eng_obj.wait_ge(sems[s], v)
                        known[s] = v
                ins = fn(eng_obj)
                if ANNOTATE:
                    ins.annotate(tag)
                if t is not None:
                    ins.then_inc(sems[t[0]], S.inc[t[0]])

        @block.tensor
        def _(e):
            emit(e, "pe")

        @block.scalar
        def _(e):
            emit(e, "act")

        @block.vector
        def _(e):
            emit(e, "dve")

        @block.gpsimd
        def _(e):
            emit(e, "pool")

        @block.sync
        def _(e):
            emit(e, "sp")

    return nc


_NC_CACHE = {}


def _piece_major(w, pieces):
    outs = []
    for subs in pieces:
        for (c0, n) in subs:
            outs.append(w[:, c0:c0 + n].reshape(NKC, 128, n).transpose(1, 0, 2).reshape(128, NKC * n))
    return np.ascontiguousarray(np.concatenate(outs, axis=1))


def kernel(x_prompt, x_sample, state_conv, state_pool, norm_g, w_in, conv_w, pool_w, pool_scale, w_out, final_g):
    f = np.float32
    x_prompt = np.asarray(x_prompt, f); x_sample = np.asarray(x_sample, f)
    state_conv = np.asarray(state_conv, f); state_pool = np.asarray(state_pool, f)
    w_in2 = _piece_major(np.asarray(w_in, f)[0], W_PIECES)
    w_out2 = _piece_major(np.asarray(w_out, f)[0], [[(0, 512)], [(512, 512)]])
    gcols = np.ascontiguousarray(np.concatenate(
        [np.asarray(norm_g, f)[0].reshape(NKC, 128).T, np.asarray(final_g, f).reshape(NKC, 128).T], axis=1))
    cw = np.ascontiguousarray(np.asarray(conv_w, f)[0].T.reshape(4, 128, 3).transpose(1, 0, 2).reshape(128, 12))
    psc = np.ascontiguousarray(np.asarray(pool_scale, f)[0].reshape(4, 128).T)
    pw = np.ascontiguousarray(np.asarray(pool_w, f)[0])

    if "nc" not in _NC_CACHE:
        _NC_CACHE["nc"] = build_program()
    nc = _NC_CACHE["nc"]

    in_maps = []
    for c in range(N_CORES):
        sl = slice(NS * c, NS * (c + 1))
        in_maps.append({
            "xp": np.ascontiguousarray(x_prompt[c]),
            "xs": np.ascontiguousarray(x_sample[sl].transpose(1, 0, 2).reshape(NS * TPOS, D)),
            "sc": np.ascontiguousarray(state_conv[0, sl].transpose(1, 0, 2).reshape(2 * NS, 512)),
            "sp": np.ascontiguousarray(state_pool[0, sl].transpose(1, 0, 2).reshape(15 * NS, 512)),
            "gcols": gcols, "w_in": w_in2, "cw": cw, "pw": pw, "psc": psc, "w_out": w_out2,
        })
    res = run_bass_kernel_spmd(nc, in_maps, core_ids=list(range(N_CORES)))
    R = res.results
    y_prompt = np.stack([R[c]["yp"] for c in range(N_CORES)], axis=0).astype(f)
    y_sample = np.concatenate(
        [R[c]["ys"].reshape(TPOS, NS, D).transpose(1, 0, 2) for c in range(N_CORES)], axis=0).astype(f)
    ncp = np.stack([R[c]["ncp"] for c in range(N_CORES)], axis=0)[None].astype(f)
    npp = np.stack([R[c]["npp"] for c in range(N_CORES)], axis=0)[None].astype(f)
    ncs = np.concatenate([R[c]["ncs"].reshape(2, NS, 512).transpose(1, 0, 2) for c in range(N_CORES)], axis=0)[None].astype(f)
    nps = np.concatenate([R[c]["nps"].reshape(15, NS, 512).transpose(1, 0, 2) for c in range(N_CORES)], axis=0)[None].astype(f)
    return (y_prompt, y_sample, ncp, npp, ncs, nps)
```

```python
import contextlib
import os
import numpy as np
import concourse.bass as bass
import concourse.mybir as mybir
from concourse.bass_utils import run_bass_kernel_spmd

F32 = mybir.dt.float32
BF16 = mybir.dt.bfloat16
AF = mybir.ActivationFunctionType
ALU = mybir.AluOpType

N_CORES = 8
D = 1024
DIN = 3072
NKC = 8
SEQ = 2048
NS = 16
TPOS = 4
EPS = 1e-6
COL_B, COL_C, COL_U, COL_ZA, COL_V, COL_ZB = 0, 512, 1024, 1536, 2048, 2560
WINDOWS = (2, 4, 8, 16)
ANNOTATE = False
STRICT = bool(os.environ.get("KSTRICT"))

W_PIECES = [[(COL_V + 128 * g, 128)] for g in (3, 2, 1, 0)] + [[(COL_ZB, 512)]] + [
    [(COL_U + 128 * j, 128), (COL_C + 128 * j, 128), (COL_ZA + 128 * j, 128), (COL_B + 128 * j, 128)]
    for j in range(4)]


def _w_in_layout():
    off = 0
    table = {}
    piece_rng = []
    for k, subs in enumerate(W_PIECES):
        p0 = off
        for (c0, n) in subs:
            for c in range(c0, c0 + n, 128):
                table[c] = (k, off + (c - c0), n)
            off += NKC * n
        piece_rng.append((p0, off))
    return table, piece_rng, off


W_TABLE, W_RNG, W_TOT = _w_in_layout()


class Sched:
    ENGS = ("pe", "act", "dve", "pool", "sp")

    def __init__(self):
        self.ops = {e: [] for e in self.ENGS}
        self.count = {}
        self.inc = {}
        self.tag = "init"
        self.last = {}
        for e in ("pe", "act", "dve", "pool"):
            self.new_sem(e, 1)

    def new_sem(self, name, inc):
        self.count[name] = 0
        self.inc[name] = inc

    def op(self, eng, fn, waits=(), sem="default"):
        if sem == "default":
            sem = eng if eng != "sp" else None
        t = None
        if sem is not None:
            self.count[sem] += self.inc[sem]
            t = (sem, self.count[sem])
        ws = []
        if STRICT and eng in ("act", "dve", "pool") and self.last.get(eng) is not None:
            ws.append(self.last[eng])

        def flat(w):
            if w is None:
                return
            if isinstance(w, list):
                for x in w:
                    flat(x)
            else:
                ws.append(w)
        for w in waits:
            flat(w)
        self.ops[eng].append((fn, ws, t, self.tag))
        if t is not None and t[0] == eng:
            self.last[eng] = t
        return t


def build_program():
    nc = bass.Bass("TRN2", target_bir_lowering=False)
    S = Sched()

    def din(name, shape):
        return nc.dram_tensor(name, list(shape), F32, kind="ExternalInput")

    def dout(name, shape):
        return nc.dram_tensor(name, list(shape), F32, kind="ExternalOutput")

    xp_h = din("xp", [SEQ, D]); xs_h = din("xs", [NS * TPOS, D])
    sc_h = din("sc", [2 * NS, 512]); sp_h = din("sp", [15 * NS, 512])
    gc_h = din("gcols", [128, 16]); win_h = din("w_in", [128, W_TOT]); cw_h = din("cw", [128, 12])
    pw_h = din("pw", [4, 128, 128]); psc_h = din("psc", [128, 4]); wout_h = din("w_out", [128, NKC * D])
    yp_h = dout("yp", [SEQ, D]); ys_h = dout("ys", [NS * TPOS, D])
    ncp_h = dout("ncp", [2, 512]); npp_h = dout("npp", [15, 512])
    ncs_h = dout("ncs", [2 * NS, 512]); nps_h = dout("nps", [15 * NS, 512])
    xp, xs_d, sc_d, sp_d = xp_h.ap(), xs_h.ap(), sc_h.ap(), sp_h.ap()
    yp, ys_d = yp_h.ap(), ys_h.ap()
    win_d = win_h.ap()
    wout_d = wout_h.ap()
    pw_v = pw_h.ap().rearrange("g c d -> c g d")

    es = contextlib.ExitStack()
    with es:
        def sb(name, shape, dt=F32):
            return es.enter_context(nc.sbuf_tensor(name, list(shape), dt))

        def ps(name, shape, dt=F32):
            return es.enter_context(nc.psum_tensor(name, list(shape), dt))

        w_in = sb("w_in_bf", [128, W_TOT], BF16)
        w_out = sb("w_out_bf", [128, 2, NKC, 512], BF16)
        pw = sb("pw_bf", [128, 4, 128], BF16)
        gb = sb("gb", [128, D]); fgb = sb("fgb", [128, D])
        gcols = sb("gcols_sb", [128, 16])
        cw = sb("cw_sb", [128, 12]); psc = sb("psc_sb", [128, 4])
        ident_bf = sb("ident_bf", [128, 128], BF16); ident_f = sb("ident_f", [128, 128])
        invcnt = sb("invcnt", [128, 16]); neghalf = sb("neghalf", [128, 1])
        tmpfix = sb("tmpfix", [128, 16])
        NXS = 8
        xt = [sb(f"xt{i}", [128, D]) for i in range(NXS)]
        xt_s = sb("xt_s", [128, D])
        xsb = [sb(f"xsb{i}", [128, D], BF16) for i in range(2)]
        hT = [sb(f"hT{i}", [128, NKC, 512], BF16) for i in range(2)]
        hT_s = sb("hT_s", [128, NKC, 64], BF16)
        NSUB = 17
        ssx = sb("ssx", [128, NSUB]); rx = sb("rx", [128, NSUB])
        ssh = sb("ssh", [128, NSUB]); rh = sb("rh", [128, NSUB])
        sa = [sb(f"sa{i}", [128, 512]) for i in range(2)]
        Ab = [sb(f"A{i}", [128, 512]) for i in range(2)]
        sa_s = sb("sa_s", [128, 64]); Ab_s = sb("Ab_s", [128, 64])
        cu_ext = [sb(f"cu_ext{j}", [128, 514]) for j in range(4)]
        cu_ext_s = sb("cu_ext_s", [128, 4, 96])
        v_ext = [sb(f"v_ext{g}", [128, 527]) for g in range(4)]
        v_ext_s = sb("v_ext_s", [128, 4, 304])
        sbz = [sb(f"sbz{g}", [128, 512]) for g in range(4)]
        sbz_s = sb("sbz_s", [128, 4, 64])
        wsb = {2: sb("s2buf", [128, 527]), 4: sb("s4buf", [128, 527]),
               8: sb("s8buf", [128, 527]), 16: sb("s16buf", [128, 527])}
        pooled = [sb(f"pooled{g}", [128, 512], BF16) for g in range(4)]
        pooledS = [sb(f"pooledS{g}", [128, 64], BF16) for g in range(4)]
        mix = sb("mix", [128, NKC, 512], BF16)
        mix_s = sb("mix_s", [128, NKC, 64], BF16)
        NH = 3
        hsb = [sb(f"hsb{i}", [128, D]) for i in range(NH)]
        junk = sb("junk", [128, D], BF16)

        pT = ps("pT", [128, 1024], BF16)
        pP = [ps(f"pP{i}", [128, 512]) for i in range(4)]
        pW = ps("pW", [128, 512])
        pO = [ps("pO0", [128, 512]), ps("pO1", [128, 512])]
        pT3 = pT[:].rearrange("p (k t) -> p k t", k=NKC)
        pO0b3 = pO[0].bitcast(BF16)[:].rearrange("p (k t) -> p k t", k=NKC)

        for i in range(NXS):
            S.new_sem(f"x{i}", 16)
        S.new_sem("xs16", 16)
        for i in range(NH):
            S.new_sem(f"y{i}", 16)
        S.new_sem("const", 16); S.new_sem("stin", 16); S.new_sem("stout", 16); S.new_sem("gbs", 16)
        S.new_sem("pwd", 16)

        def ACT(out, in_, func, waits, scale=None, accum=None):
            kw = {}
            if scale is not None:
                kw["scale"] = scale
            if accum is not None:
                kw["accum_out"] = accum
            return S.op("act", lambda e: e.activation(out=out, in_=in_, func=func, **kw), waits=waits)

        def TT(eng, out, in0, in1, op, waits):
            return S.op(eng, lambda e: e.tensor_tensor(out=out, in0=in0, in1=in1, op=op), waits=waits)

        def STT(out, in0, scalar, in1, op0, op1, waits):
            return S.op("dve", lambda e: e.scalar_tensor_tensor(out=out, in0=in0, scalar=scalar, in1=in1,
                                                                op0=op0, op1=op1), waits=waits)

        def TS(eng, out, in0, s1, s2, op0, op1, waits):
            return S.op(eng, lambda e: e.tensor_scalar(out=out, in0=in0, scalar1=s1, scalar2=s2, op0=op0, op1=op1),
                        waits=waits)

        def CP(eng, out, in_, waits):
            return S.op(eng, lambda e: e.tensor_copy(out=out, in_=in_), waits=waits)

        def MSET(eng, ap, val, waits=()):
            return S.op(eng, lambda e: e.memset(ap, val), waits=waits)

        def MM(out, lhsT, rhs, start, stop, waits=(), inc=False):
            return S.op("pe", lambda e: e.matmul(out, lhsT=lhsT, rhs=rhs, start=start, stop=stop),
                        waits=waits, sem=("pe" if inc else None))

        def TR(out, in_, ident, waits=(), inc=False):
            return S.op("pe", lambda e: e.transpose(out=out, in_=in_, identity=ident),
                        waits=waits, sem=("pe" if inc else None))

        def DMA(eng, out, in_, waits, sem):
            return S.op(eng, lambda e: e.dma_start(out=out, in_=in_), waits=waits, sem=sem)

        t_m1 = MSET("pool", ident_bf[:], 0.0)
        t_m2 = MSET("pool", ident_f[:], 0.0)
        t_nh0 = MSET("pool", neghalf[:], -0.5)
        t_identb = S.op("pool", lambda e: e.affine_select(
            out=ident_bf[:], in_=ident_bf[:], compare_op=ALU.not_equal, fill=1.0, base=0,
            pattern=[[-1, 128]], channel_multiplier=1), waits=[t_m1])
        t_identf = S.op("pool", lambda e: e.affine_select(
            out=ident_f[:], in_=ident_f[:], compare_op=ALU.not_equal, fill=1.0, base=0,
            pattern=[[-1, 128]], channel_multiplier=1), waits=[t_m2, t_nh0])

        junk_t = [ACT(junk[:, 0:1], neghalf[:, 0:1], AF.Silu, [t_nh0])]

        t_halo = None
        for g in range(4):
            MSET("dve", v_ext[g][:, 0:15], 0.0)
            MSET("dve", cu_ext[g][:, 0:2], 0.0)
        for t in range(16):
            t_halo = MSET("dve", invcnt[:, t:t + 1], 1.0 / (t + 1))
        t_dve_init = t_halo

        wt = {}
        wo_t = {}

        def issue_wpiece(k, waits=()):
            a, b_ = W_RNG[k]
            name = f"w{k}"
            S.new_sem(name, 16)
            wt[k] = DMA("pool", w_in[:, a:b_], win_d[:, a:b_], list(waits), name)

        def issue_wout(h):
            name = f"wo{h}"
            S.new_sem(name, 16)
            wo_t[h] = DMA("pool", w_out[:, h, :, :].rearrange("p k c -> p (k c)"),
                          wout_d[:, h * NKC * 512:(h + 1) * NKC * 512], [], name)

        def w_lhsT(ecol, kc):
            k, off, n = W_TABLE[ecol]
            return w_in[:, off + kc * n: off + kc * n + 128], wt[k]

        def xrows(gs):
            if gs < 16:
                return xp[gs * 128:(gs + 1) * 128, :], 128
            return xs_d[:, :], 64

        def yrows(gs):
            if gs < 16:
                return yp[gs * 128:(gs + 1) * 128, :], 128
            return ys_d[:, :], 64

        def xtile(gs):
            return xt[gs % NXS] if gs < 16 else xt_s

        x_t = {}
        h_t = {}

        def load_x(gs, extra=(), eng="sp"):
            src, m = xrows(gs)
            if gs == 16:
                x_t[gs] = DMA("sp", xt_s[0:m, :], src, list(extra), "xs16")
                return
            slot = gs % NXS
            x_t[gs] = DMA("sp", xt[slot][0:m, :], src, [h_t.get(gs - NXS)] + list(extra), f"x{slot}")

        t_gc = DMA("sp", gcols[:], gc_h.ap(), [], "gbs")
        for gs in range(4):
            load_x(gs)
        load_x(16)
        DMA("sp", cw[:], cw_h.ap(), [], "const")
        t_const = DMA("sp", psc[:], psc_h.ap(), [], "const")
        DMA("sp", hsb[0][:, 0:512], sp_d[0:128, :], [], "stin")
        DMA("sp", hsb[0][0:112, 512:1024], sp_d[128:240, :], [], "stin")
        t_stin = DMA("sp", hsb[1][0:32, 0:512], sc_d[:, :], [], "stin")

        t_pw_box = [None]

        bankP_free = [None] * 4
        bankW_free = [None]
        bankO_free = [None, None]
        bankT_free = [None]
        xsb_free = [None, None]
        hsb_free = [None] * NH
        sbz_free = {}
        saA_free = {}
        cu_roll = [t_dve_init] * 4
        v_roll = [t_dve_init] * 4
        wsb_free = {2: None, 4: None, 8: None, 16: None}
        wsb_pool_rw = {}
        fix_t = [None]
        ss_t = {}; r_t = {}; xs_t = {}; tr_t = {}; ev_t = {}
        chunk_ctr = [0]
        conv_ctr = [0]
        xs_ctr = [0]
        h_ctr = [0]
        xs_q = {}

        S.tag = "gains"
        t_gb = []
        t_fgb = []
        for which, dst, banks in ((0, gb, (pO[0], pO[1])), (1, fgb, (pP[0], pP[1]))):
            for half in range(2):
                t = None
                for k4 in range(4):
                    kc = half * 4 + k4
                    src = bass.AP(gcols, which * 8 + kc, [[16, 128], [0, 128]])
                    t = TR(banks[half][:, k4 * 128:(k4 + 1) * 128], src, ident_f[:, :],
                           waits=([t_gc, t_identf] if k4 == 0 else ()), inc=(k4 == 3))
                cs = slice(half * 512, (half + 1) * 512)
                if which == 0:
                    tc = CP("dve", dst[:, cs], banks[half][:, :], [t])
                    bankO_free[half] = tc
                    t_gb.append(tc)
                else:
                    tc = CP("dve", dst[:, cs], banks[half][:, :], [t])
                    bankP_free[half] = tc
                    t_fgb.append(tc)

        def in_sumsq(gs):
            if gs in ss_t:
                return
            _, m = xrows(gs)
            ss_t[gs] = ACT(junk[0:m, :], xtile(gs)[0:m, :], AF.Square, [x_t[gs], junk_t[0]], accum=ssx[0:m, gs:gs + 1])
            junk_t[0] = ss_t[gs]

        def in_r(gs):
            _, m = xrows(gs)
            t1 = TS("pool", rx[0:m, gs:gs + 1], ssx[0:m, gs:gs + 1], 1.0 / D, EPS, ALU.mult, ALU.add, [ss_t[gs]])
            r_t[gs] = TT("pool", rx[0:m, gs:gs + 1], rx[0:m, gs:gs + 1], neghalf[0:m, :], ALU.pow, [t1])

        def in_xs(gs, on_pool=False):
            _, m = xrows(gs)
            q = xs_ctr[0] % 2
            xs_ctr[0] += 1
            xs_q[gs] = q
            if on_pool:
                tl = None
                for hf in range(2):
                    cs = slice(hf * 512, (hf + 1) * 512)
                    t1 = TS("pool", hsb[2][0:m, cs], xtile(gs)[0:m, cs], rx[0:m, gs:gs + 1], 1.0, ALU.mult, ALU.mult,
                            [r_t[gs], hsb_free[2]])
                    tl = TT("pool", xsb[q][0:m, cs], hsb[2][0:m, cs], gb[0:m, cs], ALU.mult, [t1, t_gb, xsb_free[q]])
                xs_t[gs] = tl
                hsb_free[2] = tl
                return
            xs_t[gs] = STT(xsb[q][0:m, :], xtile(gs)[0:m, :], rx[0:m, gs:gs + 1], gb[0:m, :], ALU.mult, ALU.mult,
                           [r_t[gs], t_gb, xsb_free[q]])

        def in_transposes(gs, alt=False):
            _, m = xrows(gs)
            q = xs_q[gs]
            bank3 = pO0b3 if alt else pT3
            free = (bankO_free[0] if alt else bankT_free[0])
            t = None
            for kc in range(NKC):
                t = TR(bank3[:, kc, 0:m], xsb[q][0:m, kc * 128:(kc + 1) * 128], ident_bf[0:m, 0:m],
                       waits=([xs_t[gs], free, t_identb] if kc == 0 else ()), inc=(kc == NKC - 1))
            tr_t[gs] = t
            xsb_free[q] = t

        def in_evac(gs, dst3, sub, alt=False):
            _, m = xrows(gs)
            c0 = sub * 128
            dst = dst3[:, :, c0:c0 + m]
            if alt:
                ev_t[gs] = CP("dve", dst, pO0b3[:, :, 0:m], [tr_t[gs]])
                bankO_free[0] = ev_t[gs]
            else:
                ev_t[gs] = ACT(dst, pT3[:, :, 0:m], AF.Copy, [tr_t[gs]])
                bankT_free[0] = ev_t[gs]

        def next_in(NB, idx, on_pool=False):
            if NB is None or idx >= len(NB.subs):
                return
            gs, _ = NB.subs[idx]
            in_sumsq(gs)
            in_r(gs)
            in_xs(gs, on_pool=on_pool)

        class Blk:
            pass

        def mk(kind, b, bi, ntok, U, T, subs, hTb, mixb):
            B = Blk()
            B.kind = kind; B.b = b; B.bi = bi; B.ntok = ntok; B.U = U; B.T = T; B.subs = subs
            B.is_p = kind == "p"; B.hT = hTb; B.mix = mixb
            B.hw_c = 2 * U; B.hw_v = 15 * U
            B.pooled_t = {}; B.silu_b_t = {}; B.ya_t = {}; B.yb_t = {}
            B.pooled = pooled if B.is_p else pooledS
            return B

        Bs = [mk("p", b, b, 512, 1, 512, [(4 * b + s, s) for s in range(4)], hT[b % 2], mix) for b in range(4)]
        BS = mk("s", 4, 4, 64, 16, 4, [(16, 0)], hT_s, mix_s)

        def cuE(B, j, a, b_):
            return cu_ext[j][:, a:b_] if B.is_p else cu_ext_s[:, j, a:b_]

        def vE(B, g, a, b_):
            return v_ext[g][:, a:b_] if B.is_p else v_ext_s[:, g, a:b_]

        def sbzE(B, g):
            return sbz[g][:, 0:B.ntok] if B.is_p else sbz_s[:, g, :]

        t_state = []

        def state_in_piece(i):
            S.tag = "state_in"
            bank, getf, setf = [
                (pO[1], lambda: bankO_free[1], lambda t: bankO_free.__setitem__(1, t)),
                (pP[2], lambda: bankP_free[2], lambda t: bankP_free.__setitem__(2, t)),
                (pP[3], lambda: bankP_free[3], lambda t: bankP_free.__setitem__(3, t)),
                (pW, lambda: bankW_free[0], lambda t: bankW_free.__setitem__(0, t)),
                (pP[1], lambda: bankP_free[1], lambda t: bankP_free.__setitem__(1, t)),
            ][i]
            if i < 4:
                g = i
                TR(bank[:, 0:128], hsb[0][:, g * 128:(g + 1) * 128], ident_f[:, :],
                   waits=[t_stin, t_identf, getf()])
                t = TR(bank[:, 128:240], hsb[0][0:112, 512 + g * 128:512 + (g + 1) * 128], ident_f[0:112, 0:112], inc=True)
                tc = ACT(v_ext_s[:, g, 0:240], bank[:, 0:240], AF.Copy, [t])
                if i == 3:
                    hsb_free[0] = tc
            else:
                t = None
                for j in range(4):
                    t = TR(bank[:, j * 32:(j + 1) * 32], hsb[1][0:32, j * 128:(j + 1) * 128], ident_f[0:32, 0:32],
                           waits=([t_stin, t_identf, getf()] if j == 0 else ()), inc=(j == 3))
                tc = ACT(cu_ext_s[:, :, 0:32], bank[:, 0:128].rearrange("p (j c) -> p j c", j=4), AF.Copy, [t])
                hsb_free[1] = tc
            setf(tc)
            t_state.append(tc)

        S.tag = "b0.in"
        first = [gs for gs, _ in Bs[0].subs] + [16]
        for gs in first:
            in_sumsq(gs)
        for i, gs in enumerate(first):
            alt = (i % 2 == 1)
            if i == 2:
                issue_wpiece(0, waits=[x_t[2]])
            in_r(gs)
            if i == 3:
                for k in range(1, 4):
                    issue_wpiece(k)
                t_pw_box[0] = DMA("pool", pw[:], pw_v, [], "pwd")
            in_xs(gs)
            if gs < 16:
                in_transposes(gs, alt=alt)
                in_evac(gs, Bs[0].hT, gs % 4, alt=alt)
        for i in range(5):
            state_in_piece(i)
        S.tag = "wdma"
        issue_wpiece(4)
        issue_wpiece(5)
        for gs in range(4, 8):
            load_x(gs, extra=[wt[4] if gs < 6 else wt[5]])

        store_t = {}

        def proj_chunk(B, ecol, extra_waits=()):
            i = chunk_ctr[0] % 4
            chunk_ctr[0] += 1
            ntok = B.ntok
            t = None
            for kc in range(NKC):
                lhsT, wtick = w_lhsT(ecol, kc)
                t = MM(pP[i][:, 0:ntok], lhsT, B.hT[:, kc, 0:ntok],
                       start=(kc == 0), stop=(kc == NKC - 1),
                       waits=([bankP_free[i], wtick] + list(extra_waits) if kc == 0 else ()),
                       inc=(kc == NKC - 1))
            return i, t

        def emit_pool_v(B, g, evw):
            ntok = B.ntok
            S.tag = f"b{B.bi}.v{g}"
            i, tp = proj_chunk(B, COL_V + 128 * g, extra_waits=evw)
            tv = ACT(vE(B, g, B.hw_v, B.hw_v + ntok), pP[i][:, 0:ntok], AF.Copy,
                     [tp, v_roll[g] if B.is_p else None])
            bankP_free[i] = tv
            B.tv = getattr(B, "tv", {})
            B.tv[g] = tv

        def emit_pool_sums(B, g, pooled_now=True):
            ntok, U, T = B.ntok, B.U, B.T
            S.tag = f"b{B.bi}.v{g}"
            w = WINDOWS[g]
            need = {w: 15}
            ww = w
            while ww > 2:
                need[ww // 2] = need[ww] - ww // 2
                ww //= 2
            prev_t = [B.tv[g]] + ([] if B.is_p else list(t_state))
            for ww in sorted(need):
                lo = need[ww]; hi = 15 + T
                sh = ww // 2
                if ww == 2:
                    in0 = vE(B, g, lo * U, hi * U); in1 = vE(B, g, (lo - 1) * U, (hi - 1) * U)
                else:
                    src = wsb[ww // 2]
                    in0 = src[:, lo * U:hi * U]; in1 = src[:, (lo - sh) * U:(hi - sh) * U]
                prev_t = TT("pool", wsb[ww][:, lo * U:hi * U], in0, in1, ALU.add,
                            [prev_t, wsb_free[ww], wsb_pool_rw.get(ww)])
                wsb_pool_rw[ww] = prev_t
                if ww > 2:
                    wsb_pool_rw[ww // 2] = prev_t
            fin = wsb[w]
            if not pooled_now:
                B.wsum_t = getattr(B, "wsum_t", {})
                B.wsum_t[g] = prev_t
                return
            tpo = STT(B.pooled[g][:, 0:ntok], fin[:, 15 * U:(15 + T) * U], 1.0 / w, vE(B, g, B.hw_v, B.hw_v + ntok),
                      ALU.mult, ALU.subtract, [prev_t])
            if B.is_p and B.b == 0:
                nfx = w - 1
                tf1 = TT("dve", tmpfix[:, 0:nfx], fin[:, 15:15 + nfx], invcnt[:, 0:nfx], ALU.mult, [tpo, fix_t[0]])
                tpo = TT("dve", pooled[g][:, 0:nfx], tmpfix[:, 0:nfx], v_ext[g][:, 15:15 + nfx], ALU.subtract, [tf1])
                fix_t[0] = tpo
            B.pooled_t[g] = tpo
            wsb_free[w] = tpo
            if B.is_p and B.b < 3:
                v_roll[g] = CP("dve", v_ext[g][:, 0:15], v_ext[g][:, 512:527], [tpo])

        def emit_zb(B, g):
            S.tag = f"b{B.bi}.zB{g}"
            i, tp = proj_chunk(B, COL_ZB + 128 * g)
            ts_ = ACT(sbzE(B, g), pP[i][:, 0:B.ntok], AF.Silu, [tp, sbz_free.get((B.is_p, g))])
            bankP_free[i] = ts_
            B.silu_b_t[g] = ts_

        def emit_poolw(B, g):
            ntok = B.ntok
            S.tag = f"b{B.bi}.poolw{g}"
            tpw = MM(pW[:, 0:ntok], pw[:, g, :], B.pooled[g][:, 0:ntok], True, True,
                     waits=[B.pooled_t[g], bankW_free[0], t_pw_box[0]], inc=True)
            tyb = STT(B.mix[:, 4 + g, 0:ntok], pW[:, 0:ntok], psc[:, g:g + 1], sbzE(B, g),
                      ALU.mult, ALU.mult, [tpw, B.silu_b_t[g], t_const])
            bankW_free[0] = tyb
            sbz_free[(B.is_p, g)] = tyb
            B.yb_t[g] = tyb

        def emit_sample_pooled(B):
            ntok, U, T = B.ntok, B.U, B.T
            S.tag = "b4.pooled"
            for g in (3, 2, 1, 0):
                w = WINDOWS[g]
                tpo = STT(B.pooled[g][:, 0:ntok], wsb[w][:, 15 * U:(15 + T) * U], 1.0 / w,
                          vE(B, g, B.hw_v, B.hw_v + ntok), ALU.mult, ALU.subtract, [B.wsum_t[g]])
                B.pooled_t[g] = tpo
                wsb_free[w] = tpo

        def emit_sample_pool_tail(B):
            ntok = B.ntok
            S.tag = "b4.pooltail"
            tp = None
            for g in (3, 2, 1, 0):
                tp = MM(pW[:, g * 64:(g + 1) * 64], pw[:, g, :], B.pooled[g][:, 0:ntok], True, True,
                        waits=([bankW_free[0], t_pw_box[0]] + [B.pooled_t[q] for q in range(4)] if g == 3 else ()),
                        inc=(g == 0))
            tyb = None
            for g in (3, 2, 1, 0):
                tyb = STT(B.mix[:, 4 + g, 0:ntok], pW[:, g * 64:(g + 1) * 64], psc[:, g:g + 1], sbzE(B, g),
                          ALU.mult, ALU.mult, [tp, B.silu_b_t[g], t_const])
                B.yb_t[g] = tyb
                sbz_free[(B.is_p, g)] = tyb
            bankW_free[0] = tyb

        class ConvState:
            pass

        def conv_bufs(B):
            if B.is_p:
                k = conv_ctr[0] % 2
                conv_ctr[0] += 1
                return sa[k][:, 0:B.ntok], Ab[k][:, 0:B.ntok], ("p", k)
            return sa_s[:, :], Ab_s[:, :], ("s", 0)

        def conv_u(B, j, C):
            S.tag = f"b{B.bi}.conv{j}"
            C.sa, C.A, C.key = conv_bufs(B)
            C.body = cuE(B, j, B.hw_c, B.hw_c + B.ntok)
            iu, tpu = proj_chunk(B, COL_U + 128 * j)
            C.tu = ACT(C.body, pP[iu][:, 0:B.ntok], AF.Copy, [tpu, cu_roll[j] if B.is_p else None])
            bankP_free[iu] = C.tu

        def conv_taps12(B, j, C, tt1):
            ntok, U = B.ntok, B.U
            tt2 = STT(C.A, cuE(B, j, U, U + ntok), cw[:, 3 * j + 1:3 * j + 2], C.A, ALU.mult, ALU.add,
                      [tt1, C.tcu] + ([] if B.is_p else list(t_state)))
            C.tyc = STT(C.A, cuE(B, j, 2 * U, 2 * U + ntok), cw[:, 3 * j + 2:3 * j + 3], C.A, ALU.mult, ALU.add, [tt2])
            if B.is_p and B.b < 3:
                cu_roll[j] = CP("dve", cu_ext[j][:, 0:2], cu_ext[j][:, 512:514], [C.tyc])

        def conv_c(B, j, C):
            S.tag = f"b{B.bi}.conv{j}"
            ntok = B.ntok
            ic, tpc = proj_chunk(B, COL_C + 128 * j)
            C.tcu = TT("dve", C.body, pP[ic][:, 0:ntok], C.body, ALU.mult, [tpc, C.tu])
            bankP_free[ic] = C.tcu
            C.tt1 = None
            if j == 3 and B.is_p and B.bi > 0:
                C.tt1 = TS("pool", C.A, cuE(B, j, 0, ntok), cw[:, 3 * j:3 * j + 1], 1.0, ALU.mult, ALU.mult,
                           [C.tcu, saA_free.get(C.key), t_const])

        def conv_za(B, j, C):
            S.tag = f"b{B.bi}.conv{j}"
            iz, tpz = proj_chunk(B, COL_ZA + 128 * j)
            C.tsl = ACT(C.sa, pP[iz][:, 0:B.ntok], AF.Silu, [tpz, saA_free.get(C.key)])
            bankP_free[iz] = C.tsl
            if C.tt1 is not None:
                conv_taps12(B, j, C, C.tt1)

        def conv_b(B, j, C):
            S.tag = f"b{B.bi}.conv{j}"
            ntok = B.ntok
            ib, tpb = proj_chunk(B, COL_B + 128 * j)
            tgt = TT("dve", C.sa, pP[ib][:, 0:ntok], C.sa, ALU.mult, [tpb, C.tsl])
            bankP_free[ib] = tgt
            if C.tt1 is None:
                tt1 = ACT(C.A, cuE(B, j, 0, ntok), AF.Copy,
                          [C.tcu, saA_free.get(C.key), t_const] + ([] if B.is_p else list(t_state)),
                          scale=cw[:, 3 * j:3 * j + 1])
                conv_taps12(B, j, C, tt1)
            tya = TT("dve", B.mix[:, j, 0:ntok], C.A, C.sa, ALU.mult, [C.tyc, tgt])
            B.ya_t[j] = tya
            saA_free[C.key] = tya

        def emit_out_sub(B, gs, sub, first_sub=False):
            _, m = xrows(gs)
            c0 = sub * 128
            hk = h_ctr[0] % NH
            h_ctr[0] += 1
            xtl = xtile(gs)
            S.tag = f"b{B.bi}.out{sub}"
            korder = [7, 6, 5, 4, 0, 1, 2, 3]
            kwait = {4: B.yb_t[0], 5: B.yb_t[1], 6: B.yb_t[2], 7: B.yb_t[3],
                     0: B.ya_t[0], 1: B.ya_t[1], 2: B.ya_t[2], 3: B.ya_t[3]}
            tl = [None, None]
            if first_sub:
                seq = [(half, idx, kc) for half in range(2) for idx, kc in enumerate(korder[:-1])]
                seq += [(half, NKC - 1, korder[-1]) for half in range(2)]
            else:
                seq = [(half, idx, kc) for half in range(2) for idx, kc in enumerate(korder)]
            for (half, idx, kc) in seq:
                w = [kwait[kc]]
                if idx == 0:
                    w += [bankO_free[half], wo_t[half]]
                tl[half] = MM(pO[half][0:m, :], B.mix[:, kc, c0:c0 + m], w_out[:, half, kc, :],
                              (idx == 0), (idx == NKC - 1), waits=w, inc=(idx == NKC - 1))
            ths = []
            for half in range(2):
                th = TT("dve", hsb[hk][0:m, half * 512:(half + 1) * 512], pO[half][0:m, :],
                        xtl[0:m, half * 512:(half + 1) * 512], ALU.add, [tl[half], hsb_free[hk]])
                bankO_free[half] = th
                ths.append(th)
            h_t[gs] = ths[1]
            tsq = ACT(junk[0:m, :], hsb[hk][0:m, :], AF.Square, ths + [junk_t[0]], accum=ssh[0:m, gs:gs + 1])
            junk_t[0] = tsq
            t1 = TS("pool", rh[0:m, gs:gs + 1], ssh[0:m, gs:gs + 1], 1.0 / D, EPS, ALU.mult, ALU.add, [tsq])
            tr2 = TT("pool", rh[0:m, gs:gs + 1], rh[0:m, gs:gs + 1], neghalf[0:m, :], ALU.pow, [t1])

            def emit_yout():
                S.tag = f"b{B.bi}.yout{sub}"
                ty = STT(hsb[hk][0:m, :], hsb[hk][0:m, :], rh[0:m, gs:gs + 1], fgb[0:m, :], ALU.mult, ALU.mult,
                         [tr2, tsq, t_fgb])
                dst, _ = yrows(gs)
                tst = DMA("sp", dst, hsb[hk][0:m, :], [ty], f"y{hk}")
                hsb_free[hk] = tst
                store_t[gs] = tst
                nxt_gs = gs + NXS
                if nxt_gs < 16:
                    load_x(nxt_gs)
            return emit_yout

        def emit_state_out_prompt(B):
            S.tag = "p.state_out"
            t = None
            for g in range(4):
                t = TR(pP[0][0:15, g * 128:(g + 1) * 128], v_ext[g][:, 512:527], ident_f[:, :],
                       waits=([bankP_free[0]] + [B.pooled_t[q] for q in range(4)] if g == 0 else ()), inc=(g == 3))
            tc = ACT(xt[1][0:15, 0:512], pP[0][0:15, :], AF.Copy, [t, h_t[9]])
            bankP_free[0] = tc
            DMA("sp", npp_h.ap(), xt[1][0:15, 0:512], [tc], "stout")
            for j in range(4):
                t = TR(pP[1][0:2, j * 128:(j + 1) * 128], cu_ext[j][:, 512:514], ident_f[:, :],
                       waits=([bankP_free[1]] + [B.ya_t[q] for q in range(4)] if j == 0 else ()), inc=(j == 3))
            tc = ACT(xt[1][0:2, 512:1024], pP[1][0:2, :], AF.Copy, [t, h_t[9]])
            bankP_free[1] = tc
            DMA("sp", ncp_h.ap(), xt[1][0:2, 512:1024], [tc], "stout")

        def emit_state_out_sample(B, part):
            S.tag = "s.state_out"
            t = None
            if part == 0:
                for j in range(4):
                    t = TR(pP[2][0:32, j * 128:(j + 1) * 128], cu_ext_s[:, j, 64:96], ident_f[:, :],
                           waits=([bankP_free[2]] + [B.ya_t[q] for q in range(4)] if j == 0 else ()), inc=(j == 3))
                tc = ACT(xt[2][0:32, 0:512], pP[2][0:32, :], AF.Copy, [t, h_t[10]])
                bankP_free[2] = tc
                DMA("sp", ncs_h.ap(), xt[2][0:32, 0:512], [tc], "stout")
                for g in range(4):
                    t = TR(pP[3][:, g * 128:(g + 1) * 128], v_ext_s[:, g, 64:192], ident_f[:, :],
                           waits=([bankP_free[3]] + [B.pooled_t[q] for q in range(4)] if g == 0 else ()), inc=(g == 3))
                tc = ACT(xt[3][:, 0:512], pP[3][:, :], AF.Copy, [t, h_t[11]])
                bankP_free[3] = tc
                DMA("sp", nps_h.ap()[0:128, :], xt[3][:, 0:512], [tc], "stout")
            else:
                for g in range(4):
                    t = TR(pP[2][0:112, g * 128:(g + 1) * 128], v_ext_s[:, g, 192:304], ident_f[:, :],
                           waits=([bankP_free[2]] if g == 0 else ()), inc=(g == 3))
                tc = ACT(xt[2][0:112, 512:1024], pP[2][0:112, :], AF.Copy, [t, h_t[10]])
                bankP_free[2] = tc
                DMA("sp", nps_h.ap()[128:240, :], xt[2][0:112, 512:1024], [tc], "stout")

        for bi in range(4):
            B = Bs[bi]
            NB = Bs[bi + 1] if bi + 1 < 4 else None
            group = [B, BS] if bi == 0 else [B]
            evw = {id(X): [ev_t.get(gs) for gs, _ in X.subs] for X in group}

            for g in (3, 2, 1, 0):
                if bi > 0 and NB is not None:
                    k = {3: 0, 1: 1}.get(g)
                    if k is not None and k < len(NB.subs):
                        S.tag = f"b{bi}.nxt_in"
                        in_sumsq(NB.subs[k][0])
                    if g == 3:
                        next_in(NB, 0)
                emit_pool_v(B, g, evw[id(B)])
                if bi == 0:
                    if g == 3:
                        S.tag = "b0.in"
                        in_transposes(16)
                        in_evac(16, hT_s, 0)
                        evw[id(BS)] = [ev_t[16]]
                    else:
                        emit_pool_v(BS, g + 1, evw[id(BS)])
                emit_pool_sums(B, g)
                if bi == 0:
                    S.tag = "wdma"
                    if g == 3:
                        issue_wpiece(6)
                    elif g == 2:
                        issue_wpiece(7)
                    elif g == 1:
                        issue_wpiece(8)
                    else:
                        issue_wout(0)
                        issue_wout(1)
            if bi == 0:
                emit_pool_v(BS, 0, evw[id(BS)])
                for g in (3, 2, 1, 0):
                    emit_pool_sums(BS, g, pooled_now=False)
            for g in (3, 2, 1, 0):
                for X in group:
                    emit_zb(X, g)
                if bi > 0 and NB is not None:
                    k = {3: 2, 1: 3}.get(g)
                    if k is not None and k < len(NB.subs):
                        S.tag = f"b{bi}.nxt_in"
                        in_sumsq(NB.subs[k][0])
            if bi > 0:
                S.tag = f"b{bi}.nxt_in"
                next_in(NB, 1)

            for j in range(4):
                Cs = {id(X): ConvState() for X in group}
                for X in group:
                    conv_u(X, j, Cs[id(X)])
                emit_poolw(B, 3 - j)
                for X in group:
                    conv_c(X, j, Cs[id(X)])
                for X in group:
                    conv_za(X, j, Cs[id(X)])
                for X in group:
                    conv_b(X, j, Cs[id(X)])
                if bi == 0 and j == 3:
                    emit_sample_pooled(BS)
                S.tag = f"b{bi}.nxt_tr{j}"
                if NB is not None:
                    if bi == 0:
                        if j == 0:
                            next_in(NB, 0, True); next_in(NB, 1, True)
                        elif j == 1:
                            gs, sub = NB.subs[0]
                            in_transposes(gs); in_evac(gs, NB.hT, sub)
                            next_in(NB, 2, True)
                        elif j == 2:
                            gs, sub = NB.subs[1]
                            in_transposes(gs); in_evac(gs, NB.hT, sub)
                            next_in(NB, 3, True)
                        else:
                            gs, sub = NB.subs[2]
                            in_transposes(gs); in_evac(gs, NB.hT, sub)
                    else:
                        gs, sub = NB.subs[j]
                        in_transposes(gs); in_evac(gs, NB.hT, sub)
                        next_in(NB, j + 2)
            pend = []
            for idx, (gs, sub) in enumerate(B.subs):
                pend.append(emit_out_sub(B, gs, sub, first_sub=(idx == 0 and bi == 0)))
                if len(pend) > 1:
                    pend.pop(0)()
                if bi == 0 and idx == 0:
                    S.tag = "b0.nxt_tr3"
                    gs_l, sub_l = NB.subs[3]
                    in_transposes(gs_l); in_evac(gs_l, NB.hT, sub_l)
                    emit_sample_pool_tail(BS)
                if bi == 3 and idx in (0, 1):
                    emit_state_out_sample(BS, idx)
                if bi == 0 and idx == 1:
                    pend.append(emit_out_sub(BS, 16, 0))
                    if len(pend) > 1:
                        pend.pop(0)()
            if bi == 3:
                emit_state_out_prompt(B)
            while pend:
                pend.pop(0)()

        fin = [("stout", S.count["stout"])] + [(f"y{i}", S.count[f"y{i}"]) for i in range(NH)]
        S.op("sp", lambda e: e.nop(), waits=fin, sem=None)

        sems = {name: es.enter_context(nc.semaphore(name)) for name in S.count}
        block = es.enter_context(nc.Block())

        def emit(eng_obj, name):
            known = {}
            for fn, waits, t, tag in S.ops[name]:
                for (s, v) in waits:
                    if known.get(s, 0) < v:
                        eng_obj.wait_ge(sems[s], v)
                        known[s] = v
                ins = fn(eng_obj)
                if ANNOTATE:
                    ins.annotate(tag)
                if t is not None:
                    ins.then_inc(sems[t[0]], S.inc[t[0]])

        @block.tensor
        def _(e):
            emit(e, "pe")

        @block.scalar
        def _(e):
            emit(e, "act")

        @block.vector
        def _(e):
            emit(e, "dve")

        @block.gpsimd
        def _(e):
            emit(e, "pool")

        @block.sync
        def _(e):
            emit(e, "sp")

    return nc


_NC_CACHE = {}


def _piece_major(w, pieces):
    outs = []
    for subs in pieces:
        for (c0, n) in subs:
            outs.append(w[:, c0:c0 + n].reshape(NKC, 128, n).transpose(1, 0, 2).reshape(128, NKC * n))
    return np.ascontiguousarray(np.concatenate(outs, axis=1))


def kernel(x_prompt, x_sample, state_conv, state_pool, norm_g, w_in, conv_w, pool_w, pool_scale, w_out, final_g):
    f = np.float32
    x_prompt = np.asarray(x_prompt, f); x_sample = np.asarray(x_sample, f)
    state_conv = np.asarray(state_conv, f); state_pool = np.asarray(state_pool, f)
    w_in2 = _piece_major(np.asarray(w_in, f)[0], W_PIECES)
    w_out2 = _piece_major(np.asarray(w_out, f)[0], [[(0, 512)], [(512, 512)]])
    gcols = np.ascontiguousarray(np.concatenate(
        [np.asarray(norm_g, f)[0].reshape(NKC, 128).T, np.asarray(final_g, f).reshape(NKC, 128).T], axis=1))
    cw = np.ascontiguousarray(np.asarray(conv_w, f)[0].T.reshape(4, 128, 3).transpose(1, 0, 2).reshape(128, 12))
    psc = np.ascontiguousarray(np.asarray(pool_scale, f)[0].reshape(4, 128).T)
    pw = np.ascontiguousarray(np.asarray(pool_w, f)[0])

    if "nc" not in _NC_CACHE:
        _NC_CACHE["nc"] = build_program()
    nc = _NC_CACHE["nc"]

    in_maps = []
    for c in range(N_CORES):
        sl = slice(NS * c, NS * (c + 1))
        in_maps.append({
            "xp": np.ascontiguousarray(x_prompt[c]),
            "xs": np.ascontiguousarray(x_sample[sl].transpose(1, 0, 2).reshape(NS * TPOS, D)),
            "sc": np.ascontiguousarray(state_conv[0, sl].transpose(1, 0, 2).reshape(2 * NS, 512)),
            "sp": np.ascontiguousarray(state_pool[0, sl].transpose(1, 0, 2).reshape(15 * NS, 512)),
            "gcols": gcols, "w_in": w_in2, "cw": cw, "pw": pw, "psc": psc, "w_out": w_out2,
        })
    res = run_bass_kernel_spmd(nc, in_maps, core_ids=list(range(N_CORES)))
    R = res.results
    y_prompt = np.stack([R[c]["yp"] for c in range(N_CORES)], axis=0).astype(f)
    y_sample = np.concatenate(
        [R[c]["ys"].reshape(TPOS, NS, D).transpose(1, 0, 2) for c in range(N_CORES)], axis=0).astype(f)
    ncp = np.stack([R[c]["ncp"] for c in range(N_CORES)], axis=0)[None].astype(f)
    npp = np.stack([R[c]["npp"] for c in range(N_CORES)], axis=0)[None].astype(f)
    ncs = np.concatenate([R[c]["ncs"].reshape(2, NS, 512).transpose(1, 0, 2) for c in range(N_CORES)], axis=0)[None].astype(f)
    nps = np.concatenate([R[c]["nps"].reshape(15, NS, 512).transpose(1, 0, 2) for c in range(N_CORES)], axis=0)[None].astype(f)
    return (y_prompt, y_sample, ncp, npp, ncs, nps)
```
